# Optimizing a Trainium2 kernel written in Bass

```python
import jax, jax.numpy as jnp
from jax import lax
import numpy as np

D_MODEL = 1024
BATCH = 8
SEQ = 8192
DEPTH = 4
DEC_BATCH = 32
DEC_SEQ = 64
PAST_LEN = 1024

CHUNK = 64
N_GROUPS = 4
D_MIX = D_MODEL
GROUP_W = D_MIX // N_GROUPS
CONV_W = 31
DN_HEADS = 4
DN_HEAD_DIM = GROUP_W // DN_HEADS
DN_CONV = 4
POOL_WINDOWS = (2, 4, 8, 16)
POOL_GROUP = GROUP_W // 4
POOL_PREFIX = 15
SB_HEADS = 4
SB_HEAD_DIM = GROUP_W // SB_HEADS
SB_BLOCK = 128
SB_NEG = -1e30
D_FF = ((8 * D_MODEL // 3 + 255) // 256) * 256
N_MOD = 6
EPS = 1e-6
IN_CONV = 2 * GROUP_W
IN_DN_QKV = 3 * GROUP_W
IN_DN_GATE = GROUP_W
IN_DN_A = DN_HEADS
IN_DN_B = DN_HEADS
IN_POOL = GROUP_W
IN_SB = 3 * GROUP_W
D_IN = IN_CONV + IN_DN_QKV + IN_DN_GATE + IN_DN_A + IN_DN_B + IN_POOL + IN_SB
IN_SPLITS = (IN_CONV,
             IN_CONV + IN_DN_QKV,
             IN_CONV + IN_DN_QKV + IN_DN_GATE,
             IN_CONV + IN_DN_QKV + IN_DN_GATE + IN_DN_A,
             IN_CONV + IN_DN_QKV + IN_DN_GATE + IN_DN_A + IN_DN_B,
             IN_CONV + IN_DN_QKV + IN_DN_GATE + IN_DN_A + IN_DN_B + IN_POOL)

kernel_name = 'hybrid_streaming_encoder_step'


def rms_norm(x, g):
    xf = x.astype(jnp.float32)
    y = xf * lax.rsqrt(jnp.mean(xf * xf, axis=-1, keepdims=True) + EPS)
    return (y * g.astype(jnp.float32)).astype(x.dtype)


def l2_norm(x):
    return x * lax.rsqrt(jnp.sum(x * x, axis=-1, keepdims=True) + EPS)


def causal_depthwise(full, w):
    return lax.conv_general_dilated(full, w.astype(full.dtype)[:, None, :], window_strides=(1,),
                                    padding='VALID', dimension_numbers=('NWC', 'WIO', 'NWC'),
                                    feature_group_count=full.shape[-1])


def conformer_conv(u, prev, w_dw, b_dw, ln_g, ln_b):
    a = u[..., :GROUP_W] * jax.nn.sigmoid(u[..., GROUP_W:])
    full = jnp.concatenate([prev.astype(a.dtype), a], axis=1)
    y = (causal_depthwise(full, w_dw) + b_dw.astype(a.dtype)).astype(jnp.float32)
    mu = jnp.mean(y, axis=-1, keepdims=True)
    var = jnp.mean(jnp.square(y - mu), axis=-1, keepdims=True)
    yn = (y - mu) * lax.rsqrt(var + EPS) * ln_g.astype(jnp.float32) + ln_b.astype(jnp.float32)
    return jax.nn.silu(yn).astype(u.dtype), full[:, -(CONV_W - 1):]


def chunk_gated_delta(q, k, v, g, beta, S0):
    B, L, H, Dk = q.shape
    Dv = v.shape[-1]
    pad = (-L) % CHUNK
    if pad:
        padf = lambda t: jnp.pad(t, [(0, 0), (0, pad)] + [(0, 0)] * (t.ndim - 2))
        q, k, v, g, beta = padf(q), padf(k), padf(v), padf(g), padf(beta)
    Lp = L + pad
    N = Lp // CHUNK

    def chunks(t):
        t = t.reshape((B, N, CHUNK, H) + t.shape[3:])
        return jnp.moveaxis(t, (1, 3), (0, 2))

    qc, kc, vc, bc = chunks(q), chunks(k), chunks(v), chunks(beta)
    gc = jnp.cumsum(chunks(g), axis=-1)
    idx = jnp.arange(CHUNK)
    causal = idx[:, None] >= idx[None, :]
    strict = idx[:, None] > idx[None, :]
    decay = jnp.exp(jnp.where(causal, gc[..., :, None] - gc[..., None, :], -jnp.inf))
    kb = kc * bc[..., None]
    lmat = jnp.where(strict, jnp.einsum('nbhid,nbhjd->nbhij', kb, kc) * decay, 0.0)
    rhs = jnp.concatenate([vc * bc[..., None], kb * jnp.exp(gc)[..., None]], axis=-1)
    sol = lax.linalg.triangular_solve(lmat + jnp.eye(CHUNK, dtype=jnp.float32), rhs,
                                      left_side=True, lower=True, unit_diagonal=True)
    u_c, w_c = sol[..., :Dv], sol[..., Dv:]
    attn = jnp.einsum('nbhid,nbhjd->nbhij', qc, kc) * decay
    q_dec = qc * jnp.exp(gc)[..., None]
    k_upd = kc * jnp.exp(gc[..., -1:] - gc)[..., None]
    g_last = jnp.exp(gc[..., -1])

    def step(S, xs):
        u_i, w_i, attn_i, q_i, k_i, gl = xs
        v_new = u_i - jnp.einsum('bhcd,bhde->bhce', w_i, S)
        o = jnp.einsum('bhcd,bhde->bhce', q_i, S) + jnp.einsum('bhij,bhje->bhie', attn_i, v_new)
        S = S * gl[..., None, None] + jnp.einsum('bhcd,bhce->bhde', k_i, v_new)
        return S, o

    S, o = lax.scan(step, S0, (u_c, w_c, attn, q_dec, k_upd, g_last))
    o = jnp.moveaxis(o, (0, 2), (1, 3)).reshape(B, Lp, H, Dv)[:, :L]
    return o, S


def gated_deltanet(u_qkv, u_gate, u_a, u_b, conv_prev, S0, conv_w, a_log, dt_bias, norm_g):
    B, L, _ = u_qkv.shape
    full = jnp.concatenate([conv_prev.astype(u_qkv.dtype), u_qkv], axis=1)
    qkv = jax.nn.silu(causal_depthwise(full, conv_w)).astype(jnp.float32)
    qkv = qkv.reshape(B, L, 3, DN_HEADS, DN_HEAD_DIM)
    q = l2_norm(qkv[:, :, 0]) * (DN_HEAD_DIM ** -0.5)
    k = l2_norm(qkv[:, :, 1])
    v = qkv[:, :, 2]
    beta = jax.nn.sigmoid(u_b.astype(jnp.float32))
    g = -jnp.exp(a_log.astype(jnp.float32)) * jax.nn.softplus(u_a.astype(jnp.float32) + dt_bias.astype(jnp.float32))
    o, S = chunk_gated_delta(q, k, v, g, beta, S0.astype(jnp.float32))
    gate = jax.nn.silu(u_gate.astype(jnp.float32).reshape(B, L, DN_HEADS, DN_HEAD_DIM))
    o = rms_norm(o, norm_g) * gate
    return o.reshape(B, L, GROUP_W).astype(u_qkv.dtype), S, full[:, -(DN_CONV - 1):]


def multiscale_pool(u, prev, pos0, w_pool, scale):
    B, L, C = u.shape
    P = POOL_PREFIX
    full = jnp.concatenate([prev.astype(u.dtype), u], axis=1)
    ff = full.astype(jnp.float32)
    cs = jnp.concatenate([jnp.zeros((B, 1, C), jnp.float32), jnp.cumsum(ff, axis=1)], axis=1)
    cur = ff[:, P:]
    pos = pos0 + jnp.arange(L)
    groups = []
    for gi, w in enumerate(POOL_WINDOWS):
        lo_c, hi_c = gi * POOL_GROUP, (gi + 1) * POOL_GROUP
        win = cs[:, P + 1:P + 1 + L, lo_c:hi_c] - cs[:, P + 1 - w:P + 1 - w + L, lo_c:hi_c]
        cnt = jnp.minimum(pos + 1, w).astype(jnp.float32)
        groups.append(win / cnt[None, :, None] - cur[:, :, lo_c:hi_c])
    pooled = jnp.stack(groups, axis=2)
    y = jnp.einsum('blgc,gcd->blgd', pooled, w_pool.astype(jnp.float32)).reshape(B, L, C)
    y = y * scale.astype(jnp.float32)
    return y.astype(u.dtype), full[:, -P:]


def stick_breaking(q, k, v, q_off):
    B, H, Lq, D = q.shape
    Lk = k.shape[2]
    KB = SB_BLOCK
    QB = min(SB_BLOCK, Lq)
    nb = Lq // QB
    qf = q.astype(jnp.float32) * (D ** -0.5)
    kf = k.astype(jnp.float32)
    vf = v.astype(jnp.float32)
    kidx = jnp.arange(KB)
    upper = (kidx[:, None] >= kidx[None, :]).astype(jnp.float32)
    outs = []
    for i in range(nb):
        q0 = q_off + i * QB
        n = max(KB, -(-(q0 + QB - 1) // KB) * KB)
        ki, vi = kf[:, :, :n], vf[:, :, :n]
        if n > Lk:
            padk = [(0, 0), (0, 0), (0, n - Lk), (0, 0)]
            ki, vi = jnp.pad(ki, padk), jnp.pad(vi, padk)
        nk = n // KB
        ki = ki.reshape(B, H, nk, KB, D)
        vi = vi.reshape(B, H, nk, KB, D)
        qpos = q0 + jnp.arange(QB)
        kpos = jnp.arange(n).reshape(nk, KB)
        mask = kpos[None] < qpos[:, None, None]
        z = jnp.where(mask, jnp.einsum('bhqd,bhnkd->bhqnk', qf[:, :, i * QB:(i + 1) * QB], ki), SB_NEG)
        sp = jax.nn.softplus(z)
        c_in = jnp.einsum('bhqnj,js->bhqns', sp, upper)
        tot = c_in[..., 0]
        later = jnp.flip(jnp.cumsum(jnp.flip(tot, -1), axis=-1), -1) - tot
        a = jnp.exp(z - c_in - later[..., None])
        outs.append(jnp.einsum('bhqnk,bhnkd->bhqd', a, vi))
    return jnp.concatenate(outs, axis=2)


def stick_breaking_mixer(u, k_prev, v_prev, qn_g, kn_g):
    B, L, _ = u.shape
    qkv = u.reshape(B, L, 3, SB_HEADS, SB_HEAD_DIM).transpose(2, 0, 3, 1, 4)
    q = rms_norm(qkv[0], qn_g)
    k = rms_norm(qkv[1], kn_g)
    v = qkv[2]
    k_all = jnp.concatenate([k_prev.astype(k.dtype), k], axis=2)
    v_all = jnp.concatenate([v_prev.astype(v.dtype), v], axis=2)
    o = stick_breaking(q, k_all, v_all, k_prev.shape[2])
    o = o.transpose(0, 2, 1, 3).reshape(B, L, GROUP_W).astype(u.dtype)
    return o, k, v


def trunk_layer(x, c, pos0, conv_prev, dn_S0, dn_conv_prev, pool_prev, k_prev, v_prev,
                w_ada, b_ada, norm_mix, norm_ffn, w_in, w_out, conv_dw_w, conv_dw_b, conv_ln_g, conv_ln_b,
                dn_conv_w, dn_a_log, dn_dt_bias, dn_norm_g, pool_w, pool_scale, sb_q_norm, sb_k_norm,
                ffn_w_gate, ffn_w_up, ffn_w_down):
    B = x.shape[0]
    mod = (jax.nn.silu(c) @ w_ada + b_ada).reshape(B, N_MOD, 1, D_MODEL)
    shift1, scale1, gate1 = mod[:, 0], mod[:, 1], mod[:, 2]
    shift2, scale2, gate2 = mod[:, 3], mod[:, 4], mod[:, 5]
    h = rms_norm(x, norm_mix) * (1 + scale1) + shift1
    u = h @ w_in
    u_conv, u_dn_qkv, u_dn_gate, u_dn_a, u_dn_b, u_pool, u_sb = jnp.split(u, IN_SPLITS, axis=-1)
    y_conv, conv_new = conformer_conv(u_conv, conv_prev, conv_dw_w, conv_dw_b, conv_ln_g, conv_ln_b)
    y_dn, S_new, dn_conv_new = gated_deltanet(u_dn_qkv, u_dn_gate, u_dn_a, u_dn_b, dn_conv_prev, dn_S0,
                                              dn_conv_w, dn_a_log, dn_dt_bias, dn_norm_g)
    y_pool, pool_new = multiscale_pool(u_pool, pool_prev, pos0, pool_w, pool_scale)
    y_sb, k_new, v_new = stick_breaking_mixer(u_sb, k_prev, v_prev, sb_q_norm, sb_k_norm)
    mix = jnp.concatenate([y_conv, y_dn, y_pool, y_sb], axis=-1) @ w_out
    x = x + gate1 * mix
    h2 = rms_norm(x, norm_ffn) * (1 + scale2) + shift2
    ffn = (jax.nn.silu(h2 @ ffn_w_gate) * (h2 @ ffn_w_up)) @ ffn_w_down
    x = x + gate2 * ffn
    return x, conv_new, S_new, dn_conv_new, pool_new, k_new, v_new


def setup_inputs(seed: int = 0) -> dict:
    key = jax.random.key(seed)
    ks = jax.random.split(key, 32)
    f32 = jnp.float32
    nrm = lambda k, shape, s: jax.random.normal(k, shape, f32) * s
    dt = jnp.exp(jax.random.uniform(ks[21], (DEPTH, DN_HEADS), f32, np.log(0.001), np.log(0.1)))
    return {
        'x_prompt': nrm(ks[0], (BATCH, SEQ, D_MODEL), 1.0),
        'x_sample': nrm(ks[1], (DEC_BATCH, DEC_SEQ, D_MODEL), 1.0),
        'c_prompt': nrm(ks[2], (BATCH, D_MODEL), 1.0),
        'c_sample': nrm(ks[3], (DEC_BATCH, D_MODEL), 1.0),
        'cache_conv': nrm(ks[4], (DEPTH, DEC_BATCH, CONV_W - 1, GROUP_W), 0.5),
        'state_dn': nrm(ks[5], (DEPTH, DEC_BATCH, DN_HEADS, DN_HEAD_DIM, DN_HEAD_DIM), 0.1),
        'cache_dn_conv': nrm(ks[6], (DEPTH, DEC_BATCH, DN_CONV - 1, 3 * GROUP_W), 1.0),
        'cache_pool': nrm(ks[7], (DEPTH, DEC_BATCH, POOL_PREFIX, GROUP_W), 1.0),
        'cache_sb_k': nrm(ks[8], (DEPTH, DEC_BATCH, SB_HEADS, PAST_LEN, SB_HEAD_DIM), 1.0),
        'cache_sb_v': nrm(ks[9], (DEPTH, DEC_BATCH, SB_HEADS, PAST_LEN, SB_HEAD_DIM), 1.0),
        'w_ada': nrm(ks[10], (DEPTH, D_MODEL, N_MOD * D_MODEL), 0.5 * D_MODEL ** -0.5),
        'b_ada': nrm(ks[11], (DEPTH, N_MOD * D_MODEL), 0.01),
        'norm_mix': 1.0 + nrm(ks[12], (DEPTH, D_MODEL), 0.05),
        'norm_ffn': 1.0 + nrm(ks[13], (DEPTH, D_MODEL), 0.05),
        'w_in': nrm(ks[14], (DEPTH, D_MODEL, D_IN), D_MODEL ** -0.5),
        'w_out': nrm(ks[15], (DEPTH, D_MIX, D_MODEL), D_MIX ** -0.5),
        'conv_dw_w': nrm(ks[16], (DEPTH, CONV_W, GROUP_W), CONV_W ** -0.5),
        'conv_dw_b': nrm(ks[17], (DEPTH, GROUP_W), 0.01),
        'conv_ln_g': 1.0 + nrm(ks[18], (DEPTH, GROUP_W), 0.05),
        'conv_ln_b': nrm(ks[19], (DEPTH, GROUP_W), 0.01),
        'dn_conv_w': nrm(ks[20], (DEPTH, DN_CONV, 3 * GROUP_W), DN_CONV ** -0.5),
        'dn_a_log': jnp.log(jax.random.uniform(ks[22], (DEPTH, DN_HEADS), f32, 1.0, 16.0)),
        'dn_dt_bias': dt + jnp.log(-jnp.expm1(-dt)),
        'dn_norm_g': 1.0 + nrm(ks[23], (DEPTH, DN_HEAD_DIM), 0.05),
        'pool_w': nrm(ks[24], (DEPTH, 4, POOL_GROUP, POOL_GROUP), POOL_GROUP ** -0.5),
        'pool_scale': 1.0 + nrm(ks[25], (DEPTH, GROUP_W), 0.1),
        'sb_q_norm': 1.0 + nrm(ks[26], (DEPTH, SB_HEAD_DIM), 0.05),
        'sb_k_norm': 1.0 + nrm(ks[27], (DEPTH, SB_HEAD_DIM), 0.05),
        'ffn_w_gate': nrm(ks[28], (DEPTH, D_MODEL, D_FF), D_MODEL ** -0.5),
        'ffn_w_up': nrm(ks[29], (DEPTH, D_MODEL, D_FF), D_MODEL ** -0.5),
        'ffn_w_down': nrm(ks[30], (DEPTH, D_FF, D_MODEL), D_FF ** -0.5),
    }


def reference(x_prompt, x_sample, c_prompt, c_sample, cache_conv, state_dn, cache_dn_conv, cache_pool,
              cache_sb_k, cache_sb_v, w_ada, b_ada, norm_mix, norm_ffn, w_in, w_out, conv_dw_w, conv_dw_b,
              conv_ln_g, conv_ln_b, dn_conv_w, dn_a_log, dn_dt_bias, dn_norm_g, pool_w, pool_scale,
              sb_q_norm, sb_k_norm, ffn_w_gate, ffn_w_up, ffn_w_down):
    Bp = x_prompt.shape[0]
    dt = x_prompt.dtype
    past = cache_sb_k.shape[3]
    xp, xs = x_prompt, x_sample
    new_p = [[] for _ in range(6)]
    new_s = [[] for _ in range(6)]
    for l in range(DEPTH):
        lw = (w_ada[l], b_ada[l], norm_mix[l], norm_ffn[l], w_in[l], w_out[l], conv_dw_w[l], conv_dw_b[l],
              conv_ln_g[l], conv_ln_b[l], dn_conv_w[l], dn_a_log[l], dn_dt_bias[l], dn_norm_g[l],
              pool_w[l], pool_scale[l], sb_q_norm[l], sb_k_norm[l], ffn_w_gate[l], ffn_w_up[l], ffn_w_down[l])
        xp, *sp = trunk_layer(xp, c_prompt, 0,
                              jnp.zeros((Bp, CONV_W - 1, GROUP_W), dt),
                              jnp.zeros((Bp, DN_HEADS, DN_HEAD_DIM, DN_HEAD_DIM), jnp.float32),
                              jnp.zeros((Bp, DN_CONV - 1, 3 * GROUP_W), dt),
                              jnp.zeros((Bp, POOL_PREFIX, GROUP_W), dt),
                              jnp.zeros((Bp, SB_HEADS, 0, SB_HEAD_DIM), dt),
                              jnp.zeros((Bp, SB_HEADS, 0, SB_HEAD_DIM), dt),
                              *lw)
        xs, *ss = trunk_layer(xs, c_sample, past, cache_conv[l], state_dn[l], cache_dn_conv[l], cache_pool[l],
                              cache_sb_k[l], cache_sb_v[l], *lw)
        for i in range(6):
            new_p[i].append(sp[i])
            new_s[i].append(ss[i])
    conv_p, dn_p, dnconv_p, pool_p, sbk_p, sbv_p = [jnp.stack(a) for a in new_p]
    conv_s, dn_s, dnconv_s, pool_s, sbk_s, sbv_s = [jnp.stack(a) for a in new_s]
    return (xp, xs, conv_p, conv_s, dn_p, dn_s, dnconv_p, dnconv_s, pool_p, pool_s, sbk_p, sbk_s, sbv_p, sbv_s)
```

```python
import contextlib
import numpy as np
import concourse.bass as bass
import concourse.mybir as mybir
from concourse.bass_utils import run_bass_kernel_spmd

F32 = mybir.dt.float32
BF16 = mybir.dt.bfloat16
U8 = mybir.dt.uint8
AF = mybir.ActivationFunctionType
ALU = mybir.AluOpType
AX = mybir.AxisListType

D = 1024
DIN = 2568
DFF = 2816
NJ = 22
EPS = 1e-6
PAST = 1024
NPV = 160
SAME_SYNC = True
BIG = 30000.0
INS_TAGS = {}
STOREQ = "sp"


class Tk:
    __slots__ = ("w", "r", "ap", "excl")

    def __init__(self, ap=None, excl=False):
        self.w = None
        self.r = {}
        self.ap = ap
        self.excl = excl

    def __getitem__(self, k):
        return self.ap[k]


class Sched:
    ENG = ("pe", "act", "dve", "pool", "sp")

    def __init__(self, nc, es):
        self.nc = nc
        self.es = es
        self.ops = {e: [] for e in self.ENG}
        self.cnt = {e: 0 for e in self.ENG}
        self.seen = {e: {} for e in self.ENG}
        self.sems = {}
        self.dcnt = {}

    def sem(self, key):
        if key not in self.sems:
            self.sems[key] = self.es.enter_context(self.nc.semaphore("s_" + str(key)))
        return self.sems[key]

    def _deps(self, eng, reads, writes):
        need = {}

        def add(kv, same_ok=True):
            if kv is None:
                return
            k, v = kv
            if k == eng and (eng == "pe" or not SAME_SYNC or not same_ok):
                return
            if need.get(k, 0) < v:
                need[k] = v

        for t in reads:
            add(t.w)
        for t in writes:
            add(t.w)
            for k, v in t.r.items():
                add((k, v), same_ok=False)
        waits = []
        for k, v in need.items():
            if self.seen[eng].get(k, 0) >= v:
                continue
            self.seen[eng][k] = v
            waits.append((k, v))
        return waits

    def op(self, eng, fn, reads=(), writes=()):
        import sys as _s
        tag = (_s._getframe(2).f_lineno, _s._getframe(3).f_lineno)
        fn = self._wrap(fn, tag)
        if eng != "pe":
            ex = [t for t in reads if t.excl]
            if ex:
                reads = [t for t in reads if not t.excl]
                writes = list(writes) + ex
        waits = self._deps(eng, reads, writes)
        self.cnt[eng] += 1
        c = self.cnt[eng]
        for t in reads:
            t.r[eng] = c
        for t in writes:
            t.w = (eng, c)
            t.r = {}
        self.ops[eng].append((waits, fn, (eng, 1)))

    def _wrap(self, fn, tag):
        def f(e):
            ins = fn(e)
            try:
                INS_TAGS[str(getattr(ins, "name", None) or getattr(getattr(ins, "ins", None), "name", None))] = tag
            except Exception:
                pass
            return ins
        return f

    def dma(self, q, key, out, in_, reads=(), writes=()):
        waits = self._deps(q, reads, writes)
        v = self.dcnt.get(key, 0) + 16
        self.dcnt[key] = v
        for t in reads:
            t.r[key] = v
        for t in writes:
            t.w = (key, v)
            t.r = {}
        self.ops[q].append((waits, (lambda e: e.dma_start(out=out, in_=in_)), (key, 16)))

    def barrier(self):
        for e in self.ENG:
            waits = []
            for k in self.ENG:
                if k != e and self.cnt[k] > self.seen[e].get(k, 0):
                    waits.append((k, self.cnt[k]))
                    self.seen[e][k] = self.cnt[k]
            for k, v in self.dcnt.items():
                if v > self.seen[e].get(k, 0):
                    waits.append((k, v))
                    self.seen[e][k] = v
            self.ops[e].append((waits, None, None))

    def emit(self):
        nc = self.nc
        for k in list(self.ENG) + list(self.dcnt.keys()):
            self.sem(k)
        with nc.Block() as block:
            def mk(name):
                def body(e):
                    for waits, fn, inc in self.ops[name]:
                        for k, v in waits:
                            e.wait_ge(self.sems[k], v)
                        if fn is not None:
                            fn(e).then_inc(self.sems[inc[0]], inc[1])
                return body
            block.tensor(mk("pe"))
            block.scalar(mk("act"))
            block.vector(mk("dve"))
            block.gpsimd(mk("pool"))
            block.sync(mk("sp"))


class Arena:
    def __init__(self, nc, es, nbytes):
        self.t = es.enter_context(nc.sbuf_tensor("arena", [128, nbytes], U8))
        self.off = 0
        self.nbytes = nbytes

    def alloc(self, shape, dt):
        n = int(np.prod(shape))
        sz = n * (4 if dt == F32 else 2)
        ap = self.t[:, self.off:self.off + sz].bitcast(dt)
        self.off += (sz + 63) // 64 * 64
        assert self.off <= self.nbytes, ("arena overflow", self.off)
        if len(shape) == 2:
            ap = ap.rearrange("p (a b) -> p a b", a=shape[0])
        elif len(shape) == 3:
            ap = ap.rearrange("p (a b c) -> p a b c", a=shape[0], b=shape[1])
        return Tk(ap)


def host_consts():
    p = np.arange(128)
    c = {}
    c["ident"] = np.eye(128, dtype=np.float32)
    c["ones"] = np.ones((128, 128), np.float32)
    same = (p[:, None] // 64) == (p[None, :] // 64)
    c["ublk"] = (same & (p[:, None] <= p[None, :])).astype(np.float32)
    c["blk64"] = same.astype(np.float32)
    c["pmL"] = np.where(same & (p[:, None] > p[None, :]), 0.0, BIG).astype(np.float32)
    c["nmU"] = np.where(same & (p[None, :] >= p[:, None]), 0.0, -BIG).astype(np.float32)
    c["upper"] = (p[:, None] >= p[None, :]).astype(np.float32)
    f = np.arange(512)
    c["sbmask"] = np.stack([((m * 128 + p[:, None]) < f[None, :]).astype(np.float32) for m in range(4)], 1)
    c["chunkind"] = np.stack([(p // 64 == 0), (p // 64 == 1)], 1).astype(np.float32)
    wins = np.array([[2, 4], [8, 16]])
    wpc = wins[:, p // 64].T.astype(np.float32)
    c["invw"] = (1.0 / wpc).astype(np.float32)
    t = np.arange(16)
    c["invcnt0"] = (1.0 / np.minimum(t[None, None, :] + 1, wpc[:, :, None])).astype(np.float32)
    return c


CONST_SHAPES = dict(ident=[128, 128], ones=[128, 128], ublk=[128, 128], blk64=[128, 128], pmL=[128, 128],
                    nmU=[128, 128], upper=[128, 128], sbmask=[128, 4, 512], chunkind=[128, 2],
                    invw=[128, 2], invcnt0=[128, 2, 16])


def build(SEQ, DEPTH, dbg=None):
    NU = SEQ // 512
    NTOK = SEQ + 256
    nc = bass.Bass("TRN2", target_bir_lowering=False)
    es = contextlib.ExitStack()

    def din(name, shape, dt=F32):
        return nc.dram_tensor(name, list(shape), dt, kind="ExternalInput").ap()

    def dout(name, shape):
        return nc.dram_tensor(name, list(shape), F32, kind="ExternalOutput").ap()

    def dscr(name, shape, dt):
        return nc.dram_tensor(name, list(shape), dt, kind="Internal").ap()

    xp = din("xp", [SEQ, D]); xs = din("xs", [256, D]); c5 = din("c5", [5, D])
    cache_conv = din("cache_conv", [DEPTH, 4, 30, 256]); state_dn = din("state_dn", [DEPTH, 4, 4, 64, 64])
    cache_dnc = din("cache_dn_conv", [DEPTH, 4, 3, 768]); cache_pool = din("cache_pool", [DEPTH, 4, 15, 256])
    cache_k = din("cache_sb_k", [DEPTH, 4, 4, PAST, 64]); cache_v = din("cache_sb_v", [DEPTH, 4, 4, PAST, 64])
    w_ada = din("w_ada", [DEPTH, D, 6 * D]); w_in = din("w_in", [DEPTH, D, DIN]); w_out = din("w_out", [DEPTH, D, D])
    w_g = din("ffn_w_gate", [DEPTH, D, DFF]); w_u = din("ffn_w_up", [DEPTH, D, DFF]); w_d = din("ffn_w_down", [DEPTH, DFF, D])
    pool_w = din("pool_w", [DEPTH, 4, 64, 64])
    pvrows = din("pvrows", [DEPTH, NPV, 128]); rowb_d = din("rowb", [DEPTH, 128, 264])
    cd = {k: din("c_" + k, v) for k, v in CONST_SHAPES.items()}

    yp = dout("yp", [SEQ, D]); ys = dout("ys", [256, D])
    o_conv_p = dout("conv_p", [DEPTH, 30, 256]); o_conv_s = dout("conv_s", [DEPTH, 4, 30, 256])
    o_dn_p = dout("dn_p", [DEPTH, 4, 64, 64]); o_dn_s = dout("dn_s", [DEPTH, 4, 4, 64, 64])
    o_dnc_p = dout("dnconv_p", [DEPTH, 3, 768]); o_dnc_s = dout("dnconv_s", [DEPTH, 4, 3, 768])
    o_pool_p = dout("pool_p", [DEPTH, 15, 256]); o_pool_s = dout("pool_s", [DEPTH, 4, 15, 256])
    o_k_p = dout("sbk_p", [DEPTH, 4, SEQ, 64]); o_k_s = dout("sbk_s", [DEPTH, 4, 4, 64, 64])
    o_v_p = dout("sbv_p", [DEPTH, 4, SEQ, 64]); o_v_s = dout("sbv_s", [DEPTH, 4, 4, 64, 64])

    xT = dscr("xT", [D, NTOK], F32)
    wb_in = dscr("wb_in", [DEPTH, 128, 8, DIN], BF16); wb_out = dscr("wb_out", [DEPTH, 128, 8, D], BF16)
    wb_g = dscr("wb_g", [DEPTH, NJ, 128, 8, 128], BF16); wb_u = dscr("wb_u", [DEPTH, NJ, 128, 8, 128], BF16)
    wb_d = dscr("wb_d", [DEPTH, NJ, 128, D], BF16); wb_ada = dscr("wb_ada", [DEPTH, 48, 128, 8, 128], BF16)
    mixq = dscr("mixq", [128, 8, NTOK], BF16)
    kT_scr = dscr("kT_scr", [128, 2, NTOK], BF16)
    v_scr = dscr("v_scr", [NTOK, 256], BF16)
    xT_v = xT.rearrange("(k p) t -> p k t", p=128)

    S = Sched(nc, es)
    A = Arena(nc, es, 212000)
    PS = [Tk(es.enter_context(nc.psum_tensor("ps%d" % i, [128, 512], F32)), excl=True) for i in range(7)]
    PSB = Tk(es.enter_context(nc.psum_tensor("psb", [128, 1024], BF16)), excl=True)
    PS_DED = PS[6]
    psi = [0]

    def bank():
        b = PS[psi[0] % 6]
        psi[0] += 1
        return b

    def bfv(b):
        return b.ap

    def mm(out, lhsT, rhs, start=True, stop=True, reads=(), writes=()):
        S.op("pe", lambda e: e.matmul(out, lhsT=lhsT, rhs=rhs, start=start, stop=stop), reads, writes)

    def tr(out, in_, ident, reads=(), writes=()):
        S.op("pe", lambda e: e.transpose(out, in_, ident), reads, writes)

    def act(out, in_, func, bias=None, scale=None, reads=(), writes=(), eng="act"):
        kw = {}
        if bias is not None:
            kw["bias"] = bias
        if scale is not None:
            kw["scale"] = scale
        S.op("act", lambda e: e.activation(out=out, in_=in_, func=func, **kw), reads, writes)

    def tt(out, in0, in1, op, reads=(), writes=(), eng="dve"):
        S.op(eng, lambda e: e.tensor_tensor(out=out, in0=in0, in1=in1, op=op), reads, writes)

    def ts(out, in0, s1, op0, s2=None, op1=None, reads=(), writes=(), eng="dve"):
        if op1 is None:
            S.op(eng, lambda e: e.tensor_scalar(out=out, in0=in0, scalar1=s1, scalar2=None, op0=op0), reads, writes)
        else:
            S.op(eng, lambda e: e.tensor_scalar(out=out, in0=in0, scalar1=s1, scalar2=s2, op0=op0, op1=op1), reads, writes)

    def stt(out, in0, scalar, in1, op0, op1, reads=(), writes=()):
        S.op("dve", lambda e: e.scalar_tensor_tensor(out=out, in0=in0, scalar=scalar, in1=in1, op0=op0, op1=op1), reads, writes)

    def cp(out, in_, reads=(), writes=(), eng="dve"):
        S.op(eng, lambda e: e.tensor_copy(out=out, in_=in_), reads, writes)

    def recip(out, in_, reads=(), writes=()):
        S.op("dve", lambda e: e.reciprocal(out=out, in_=in_), reads, writes)

    def mset(ap, val, writes=(), eng="pool"):
        S.op(eng, lambda e: e.memset(ap, val), (), writes)

    def red(out, in_, reads=(), writes=()):
        S.op("dve", lambda e: e.tensor_reduce(out=out, in_=in_, axis=AX.X, op=ALU.add), reads, writes)

    dkeys = [0]

    def dma(q, out, in_, reads=(), writes=(), key=None):
        q = STOREQ if q == "pool" else q
        if key is None:
            key = "d%d" % (dkeys[0] % 24)
            dkeys[0] += 1
        S.dma(q, key, out, in_, reads, writes)

    T_xT = [Tk() for _ in range(NU + 4)]
    T_w = Tk(); T_mixq = [Tk() for _ in range(NU + 4)]; T_kv = Tk(); T_out = Tk()

    C = {}
    for k, shp in CONST_SHAPES.items():
        if k != "sbmask":
            C[k] = A.alloc(shp[1:], F32)
    CB = {k: A.alloc(CONST_SHAPES[k][1:], BF16) for k in ("ident", "ones", "blk64", "upper", "sbmask")}
    pv = A.alloc([DEPTH, NPV], F32)
    mod = A.alloc([DEPTH * 48, 5], F32)
    gs = A.alloc([DEPTH * 16, 5], F32)
    rowb = A.alloc([DEPTH, 264], F32)
    negA = A.alloc([DEPTH, 4], F32)
    scT = A.alloc([8, 5], BF16)
    A_persist = A.off
    C["sbmask"] = A.alloc(CONST_SHAPES["sbmask"][1:], F32)

    for k in CONST_SHAPES:
        dma("sp", C[k].ap, cd[k], writes=[C[k]])
    for k in CB:
        cp(CB[k].ap, C[k].ap, reads=[C[k]], writes=[CB[k]])
    for l in range(DEPTH):
        dma("sp", rowb.ap[:, l, :], rowb_d[l], writes=[rowb])
    act(negA.ap, rowb.ap[:, :, 260:264], AF.Exp, reads=[rowb], writes=[negA])
    ts(negA.ap, negA.ap, -1.0, ALU.mult, reads=[negA], writes=[negA])

    st_a = A.alloc([NPV, 128], F32)
    for l in range(DEPTH):
        for (r0, nr) in ((0, 128), (128, NPV - 128)):
            dma("sp", st_a.ap[0:nr, 0, :], pvrows[l, r0:r0 + nr, :], writes=[st_a])
            b = bank()
            tr(b.ap[:, 0:nr], st_a.ap[0:nr, 0, :], C["ident"].ap[0:nr, 0:nr], reads=[st_a, C["ident"]], writes=[b])
            cp(pv.ap[:, l, r0:r0 + nr], b.ap[:, 0:nr], reads=[b], writes=[pv])
    dma("sp", st_a.ap[0:5, 0:8, :], c5.rearrange("b (k f) -> b k f", k=8), writes=[st_a])
    act(st_a.ap[0:5, 8:16, :], st_a.ap[0:5, 0:8, :], AF.Silu, reads=[st_a], writes=[st_a])
    b = bank()
    for k in range(8):
        tr(b.ap[:, k * 8:k * 8 + 5], st_a.ap[0:5, 8 + k, :], C["ident"].ap[0:5, 0:5], reads=[st_a, C["ident"]], writes=[b])
    cp(scT.ap, b.ap[:, 0:64].rearrange("p (k e) -> p k e", e=8)[:, :, 0:5], reads=[b], writes=[scT])
    S.barrier()
    A.off = A_persist

    stg = [A.alloc([4096], F32) for _ in range(3)]
    stb = [A.alloc([4096], BF16) for _ in range(3)]
    wi = [0]

    def wcast(dst, src, n):
        i = wi[0] % 3
        wi[0] += 1
        sh = src.shape
        sview = stg[i].ap[:, 0:n]
        bview = stb[i].ap[:, 0:n]
        if len(sh) == 3:
            sview = sview.rearrange("p (a b) -> p a b", a=sh[1]); bview = bview.rearrange("p (a b) -> p a b", a=sh[1])
        elif len(sh) == 4:
            sview = sview.rearrange("p (a b c) -> p a b c", a=sh[1], b=sh[2]); bview = bview.rearrange("p (a b c) -> p a b c", a=sh[1], b=sh[2])
        dma("sp", sview, src, writes=[stg[i]], key="wl%d" % i)
        eng = ("act", "dve", "pool")[i]
        if eng == "act":
            act(stb[i].ap[:, 0:n], stg[i].ap[:, 0:n], AF.Copy, reads=[stg[i]], writes=[stb[i]])
        else:
            cp(stb[i].ap[:, 0:n], stg[i].ap[:, 0:n], reads=[stg[i]], writes=[stb[i]], eng=eng)
        dma("pool", dst, bview, reads=[stb[i]], writes=[T_w], key="ws%d" % i)

    for l in range(DEPTH):
        for k in range(8):
            wcast(wb_in[l, :, k, :], w_in[l, k * 128:(k + 1) * 128, :], DIN)
        for k in range(0, 8, 4):
            wcast(wb_out[l, :, k:k + 4, :], w_out[l].rearrange("(k p) c -> p k c", p=128)[:, k:k + 4, :], 4096)
        for j in range(NJ):
            wcast(wb_g[l, j], w_g[l].rearrange("(k p) (j c) -> p j k c", p=128, c=128)[:, j], 1024)
            wcast(wb_u[l, j], w_u[l].rearrange("(k p) (j c) -> p j k c", p=128, c=128)[:, j], 1024)
        for j in range(0, NJ, 2):
            wcast(wb_d[l, j:j + 2].rearrange("j p c -> p j c"), w_d[l].rearrange("(j p) c -> p j c", p=128)[:, j:j + 2, :], 2048)
        for j in range(48):
            wcast(wb_ada[l, j], w_ada[l].rearrange("(k p) (j c) -> p j k c", p=128, c=128)[:, j], 1024)
    S.barrier()
    A.off = A_persist

    wa = [A.alloc([8, 128], BF16) for _ in range(3)]
    for l in range(DEPTH):
        for j in range(48):
            w = wa[j % 3]
            dma("sp", w.ap, wb_ada[l, j], reads=[T_w], writes=[w], key="wa%d" % (j % 3))
            b = bank()
            for k in range(8):
                mm(b.ap[:, 0:5], w.ap[:, k, :], scT.ap[:, k, :], start=(k == 0), stop=(k == 7), reads=[w, scT], writes=[b])
            ts(mod.ap[:, l * 48 + j, :], b.ap[:, 0:5], pv.ap[:, l, j:j + 1], ALU.add, reads=[b, pv], writes=[mod])
        for sub in range(2):
            for k in range(8):
                ts(gs.ap[:, l * 16 + sub * 8 + k, :], mod.ap[:, l * 48 + (1 + 3 * sub) * 8 + k, :], 1.0, ALU.add,
                   pv.ap[:, l, 48 + sub * 8 + k:49 + sub * 8 + k], ALU.mult, reads=[mod, pv], writes=[gs])

    def modc(l, m, k, bq):
        return mod.ap[:, l * 48 + m * 8 + k, bq:bq + 1]

    xin = [A.alloc([1024], F32) for _ in range(2)]
    xo = [A.alloc([8, 128], F32) for _ in range(2)]
    ntile = NTOK // 128
    for t in range(ntile):
        src = xp[t * 128:(t + 1) * 128, :] if t < SEQ // 128 else xs[(t - SEQ // 128) * 128:(t - SEQ // 128 + 1) * 128, :]
        xi = xin[t % 2]; xq = xo[t % 2]
        dma("sp", xi.ap, src, writes=[xi], key="xi%d" % (t % 2))
        for hlf in range(2):
            b = bank()
            for k in range(4):
                tr(b.ap[:, k * 128:(k + 1) * 128], xi.ap[:, (hlf * 4 + k) * 128:(hlf * 4 + k + 1) * 128], C["ident"].ap,
                   reads=[xi, C["ident"]], writes=[b])
            if hlf == 0:
                cp(xq.ap[:, 0:4, :], b.ap[:, :].rearrange("p (k c) -> p k c", k=4), reads=[b], writes=[xq])
            else:
                act(xq.ap[:, 4:8, :], b.ap[:, :].rearrange("p (k c) -> p k c", k=4), AF.Copy, reads=[b], writes=[xq])
        dma("pool", xT_v[:, :, t * 128:(t + 1) * 128], xq.ap, reads=[xq], writes=T_xT, key="xo%d" % (t % 2))
    S.barrier()
    A.off = A_persist

    units = [(s + 1, SEQ + s * 64, 64, True, True, NU + s) for s in range(4)]
    units += [(0, u * 512, 512, u == 0, u == NU - 1, u) for u in range(NU)]

    epsc = A.alloc([4], F32)
    mset(epsc.ap[:, 0:1], EPS, writes=[epsc])
    mset(epsc.ap[:, 1:2], 64 * EPS, writes=[epsc])
    mset(epsc.ap[:, 2:3], 1.0, writes=[epsc])
    A_persist = A.off

    def rms_h(l, sub, bq, x, N, hT, x2, rs, tmps):
        b = bank()
        for k in range(8):
            xq_ = x2[k % 2]
            act(xq_.ap[:, 0:N], x.ap[:, k, 0:N], AF.Square, reads=[x], writes=[xq_])
            mm(b.ap[:, 0:N], CB["ones"].ap, xq_.ap[:, 0:N], start=(k == 0), stop=(k == 7), reads=[xq_, CB["ones"]], writes=[b])
        act(rs.ap[:, 0:N], b.ap[:, 0:N], AF.Sqrt, bias=epsc.ap[:, 0:1], scale=1.0 / D, reads=[b, epsc], writes=[rs])
        recip(rs.ap[:, 0:N], rs.ap[:, 0:N], reads=[rs], writes=[rs])
        for k in range(8):
            t = tmps[k % 2]
            stt(t.ap[:, 0:N], x.ap[:, k, 0:N], gs.ap[:, l * 16 + sub * 8 + k, bq:bq + 1], rs.ap[:, 0:N], ALU.mult, ALU.mult,
                reads=[x, gs, rs], writes=[t])
            act(hT.ap[:, k, 0:N], t.ap[:, 0:N], AF.Identity, bias=modc(l, 3 * sub, k, bq), scale=1.0, reads=[t, mod], writes=[hT])

    def layer_B(l):
        S.barrier()
        A.off = A_persist
        G = 2
        xg = [A.alloc([8, 512], F32) for _ in range(G)]
        hT = [A.alloc([8, 512], BF16) for _ in range(G)]
        hid = [A.alloc([NJ, 512], BF16) for _ in range(G)]
        wd = A.alloc([NJ, D], BF16)
        wgb = [A.alloc([8, 128], BF16) for _ in range(3)]
        wub = [A.alloc([8, 128], BF16) for _ in range(3)]
        x2 = [A.alloc([512], BF16) for _ in range(2)]
        rs = A.alloc([512], F32)
        tmps = [A.alloc([512], F32) for _ in range(2)]
        sg = [A.alloc([512], F32) for _ in range(2)]
        dma("sp", wd.ap, wb_d[l].rearrange("j p c -> p j c"), reads=[T_w], writes=[wd], key="wd")
        groups = [units[i:i + G] for i in range(0, len(units), G)]
        wc = 0
        for grp in groups:
            for i, (bq, col0, N, first, last, ui) in enumerate(grp):
                dma("sp", xg[i].ap[:, :, 0:N], xT_v[:, :, col0:col0 + N], reads=[T_xT[ui]], writes=[xg[i]], key="bx%d" % i)
                rms_h(l, 1, bq, xg[i], N, hT[i], x2, rs, tmps)
            for j in range(NJ):
                wg_ = wgb[wc % 3]; wu_ = wub[wc % 3]
                dma("sp", wg_.ap, wb_g[l, j], reads=[T_w], writes=[wg_], key="wg%d" % (wc % 3))
                dma("sp", wu_.ap, wb_u[l, j], reads=[T_w], writes=[wu_], key="wu%d" % (wc % 3))
                wc += 1
                for i, (bq, col0, N, first, last, ui) in enumerate(grp):
                    pg = bank(); pu = bank()
                    for k in range(8):
                        mm(pg.ap[:, 0:N], wg_.ap[:, k, :], hT[i].ap[:, k, 0:N], start=(k == 0), stop=(k == 7), reads=[wg_, hT[i]], writes=[pg])
                    for k in range(8):
                        mm(pu.ap[:, 0:N], wu_.ap[:, k, :], hT[i].ap[:, k, 0:N], start=(k == 0), stop=(k == 7), reads=[wu_, hT[i]], writes=[pu])
                    s_ = sg[(j * G + i) % 2]
                    act(s_.ap[:, 0:N], pg.ap[:, 0:N], AF.Silu, reads=[pg], writes=[s_])
                    tt(hid[i].ap[:, j, 0:N], s_.ap[:, 0:N], pu.ap[:, 0:N], ALU.mult, reads=[s_, pu], writes=[hid[i]])
            for i, (bq, col0, N, first, last, ui) in enumerate(grp):
                for m in range(8):
                    po = bank()
                    for j in range(NJ):
                        mm(po.ap[:, 0:N], wd.ap[:, j, m * 128:(m + 1) * 128], hid[i].ap[:, j, 0:N], start=(j == 0), stop=(j == NJ - 1),
                           reads=[wd, hid[i]], writes=[po])
                    stt(xg[i].ap[:, m, 0:N], po.ap[:, 0:N], modc(l, 5, m, bq), xg[i].ap[:, m, 0:N], ALU.mult, ALU.add,
                        reads=[po, mod, xg[i]], writes=[xg[i]])
                dma("pool", xT_v[:, :, col0:col0 + N], xg[i].ap[:, :, 0:N], reads=[xg[i]], writes=[T_xT[ui]], key="bo%d" % i)

    def layer_A1(l):
        S.barrier()
        A.off = A_persist
        al = A.alloc
        win = al([8, DIN], BF16)
        cdg = al([62, 128], BF16); ddg = al([24, 128], BF16)
        wpf = al([2, 128], F32); wpb = al([2, 128], BF16)
        x = al([8, 512], F32); hT = al([8, 512], BF16)
        x2 = [al([512], BF16) for _ in range(2)]
        rs = al([512], F32); tmps = [al([512], F32) for _ in range(2)]
        mixb = al([8, 512], BF16)
        aext = al([2, 544], BF16); pext = al([2, 528], F32); dnext = al([6, 516], BF16)
        S32 = al([2, 128], F32); Sbf = al([2, 128], BF16)
        stg = al([768], F32)
        f1 = [al([512], F32) for _ in range(6)]
        b1 = [al([512], BF16) for _ in range(3)]
        ycv = al([2, 512], F32)
        dqk = al([6, 512], BF16)
        ktb = al([2, 512], BF16); vtb = al([2, 512], BF16)
        s1 = al([528], F32); s2 = al([528], F32); s3 = s1; s4 = s2
        ktok = al([256], F32); vtok = al([256], F32); vtokb = al([256], BF16)
        sm = {k: al([8], F32) for k in ("ab", "beta", "g", "gc", "gtot", "egc", "ekd", "t4", "ssq", "rr")}
        XL = al([4, 128], F32); dg = XL; XU = al([4, 128], F32); EL = XL; EU = XU
        ER = XL
        Mb = [al([4, 128], F32) for _ in range(2)]; MT = [al([4, 128], F32) for _ in range(6)]
        attnT = al([4, 128], BF16); r32 = al([4, 128], F32); rbf = r32
        wtok = al([2, 128], BF16); wT = al([2, 128], BF16); qdT = al([2, 128], BF16); kupd = al([2, 128], BF16)
        vnew = al([256], BF16); Ghd = al([2, 128], F32); glast = al([2, 2], F32)
        kvt = al([512], BF16)
        otok = f1[0]; osq = f1[2]; sgate = f1[3]; ytok = al([256], BF16); tmpS = al([2, 128], F32)
        identF = C["ident"].ap; identB = CB["ident"].ap

        dma("sp", win.ap, wb_in[l], reads=[T_w], writes=[win], key="win")
        for j in range(31):
            for c in range(2):
                ts(cdg.ap[:, j * 2 + c, :], identF, pv.ap[:, l, 74 + j * 2 + c:75 + j * 2 + c], ALU.mult, reads=[C["ident"], pv], writes=[cdg])
        for j in range(4):
            for i in range(6):
                ts(ddg.ap[:, j * 6 + i, :], identF, pv.ap[:, l, 136 + j * 6 + i:137 + j * 6 + i], ALU.mult, reads=[C["ident"], pv], writes=[ddg])
        mset(wpf.ap, 0.0, writes=[wpf])
        for gq in range(4):
            h0 = (gq % 2) * 64
            dma("sp", wpf.ap[h0:h0 + 64, gq // 2, h0:h0 + 64], pool_w[l, gq], writes=[wpf], key="wp")
        cp(wpb.ap, wpf.ap, reads=[wpf], writes=[wpb])

        def proj(c0, ncol, N):
            b = bank()
            for k in range(8):
                mm(b.ap[0:ncol, 0:N], win.ap[:, k, c0:c0 + ncol], hT.ap[:, k, 0:N], start=(k == 0), stop=(k == 7), reads=[win, hT], writes=[b])
            return b

        def load_prefix(src, nrow, ncol, dst, c0, nchunk, isbf):
            dma("sp", stg.ap[0:nrow, 0:ncol], src, writes=[stg], key="pf")
            for i in range(nchunk):
                b = bank()
                tr(b.ap[:, 0:nrow], stg.ap[0:nrow, i * 128:(i + 1) * 128], identF[0:nrow, 0:nrow], reads=[stg, C["ident"]], writes=[b])
                cp(dst.ap[:, i, c0:c0 + nrow], b.ap[:, 0:nrow], reads=[b], writes=[dst])

        def store_tail(dst_dram, nrow, src, cs, nchunk, isbf, r0=0):
            for i in range(nchunk):
                b = PSB if isbf else bank()
                if isbf:
                    tr(bfv(b)[0:nrow, 0:128], src.ap[:, i, cs:cs + nrow], identB, reads=[src, CB["ident"]], writes=[b])
                    cp(stg.ap[0:nrow, i * 128:(i + 1) * 128], bfv(b)[0:nrow, 0:128], reads=[b], writes=[stg])
                else:
                    tr(b.ap[0:nrow, 0:128], src.ap[:, i, cs:cs + nrow], identF, reads=[src, C["ident"]], writes=[b])
                    cp(stg.ap[0:nrow, i * 128:(i + 1) * 128], b.ap[0:nrow, 0:128], reads=[b], writes=[stg])
            dma("pool", dst_dram, stg.ap[r0:nrow, 0:nchunk * 128], reads=[stg], writes=[T_out], key="tl")

        def headnorm(src, N, bias_col, scale, dst, gcol=None):
            sq = b1[0]
            tt(sq.ap[:, 0:N], src.ap[:, 0:N], src.ap[:, 0:N], ALU.mult, reads=[src], writes=[sq])
            b = bank()
            mm(b.ap[:, 0:N], CB["blk64"].ap, sq.ap[:, 0:N], reads=[sq, CB["blk64"]], writes=[b])
            rt = f1[5]
            act(rt.ap[:, 0:N], b.ap[:, 0:N], AF.Sqrt, bias=bias_col, scale=scale, reads=[b, epsc], writes=[rt])
            recip(rt.ap[:, 0:N], rt.ap[:, 0:N], reads=[rt], writes=[rt])
            if gcol is None:
                tt(dst, src.ap[:, 0:N], rt.ap[:, 0:N], ALU.mult, reads=[src, rt], writes=[])
            else:
                stt(dst, src.ap[:, 0:N], gcol, rt.ap[:, 0:N], ALU.mult, ALU.mult, reads=[src, rt, pv], writes=[])

        for (bq, col0, N, first, last, ui) in units:
            sidx = bq - 1
            if first:
                mset(aext.ap[:, :, 0:32], 0.0, writes=[aext]); mset(pext.ap[:, :, 0:16], 0.0, writes=[pext])
                mset(dnext.ap[:, :, 0:4], 0.0, writes=[dnext]); mset(S32.ap, 0.0, writes=[S32])
                if bq > 0:
                    load_prefix(cache_conv[l, sidx], 30, 256, aext, 2, 2, True)
                    load_prefix(cache_pool[l, sidx], 15, 256, pext, 1, 2, False)
                    load_prefix(cache_dnc[l, sidx], 3, 768, dnext, 1, 6, True)
                    for h in range(4):
                        h0 = (h % 2) * 64
                        dma("sp", S32.ap[h0:h0 + 64, h // 2, h0:h0 + 64], state_dn[l, sidx, h], writes=[S32], key="sd")
                cp(Sbf.ap, S32.ap, reads=[S32], writes=[Sbf])
            else:
                cp(aext.ap[:, :, 0:32], aext.ap[:, :, 512:544], reads=[aext], writes=[aext])
                cp(pext.ap[:, :, 0:16], pext.ap[:, :, 512:528], reads=[pext], writes=[pext])
                cp(dnext.ap[:, :, 0:4], dnext.ap[:, :, 512:516], reads=[dnext], writes=[dnext])
            dma("sp", x.ap[:, :, 0:N], xT_v[:, :, col0:col0 + N], reads=[T_xT[ui]], writes=[x], key="ax")
            rms_h(l, 0, bq, x, N, hT, x2, rs, tmps)
            parts = (dbg or {}).get('parts', 'conv pool sb dn')
            T = min(128, N)
            if 'conv' in parts:
                for c in range(2):
                    pvv = proj(c * 128, 128, N); pgg = proj(256 + c * 128, 128, N)
                    sg_ = f1[c]
                    act(sg_.ap[:, 0:N], pgg.ap[:, 0:N], AF.Sigmoid, reads=[pgg], writes=[sg_])
                    tt(aext.ap[:, c, 32:32 + N], pvv.ap[:, 0:N], sg_.ap[:, 0:N], ALU.mult, reads=[pvv, sg_], writes=[aext])
                for c in range(2):
                    b = bank()
                    for j in range(31):
                        mm(b.ap[:, 0:N], cdg.ap[:, j * 2 + c, :], aext.ap[:, c, 2 + j:2 + j + N], start=(j == 0), stop=(j == 30), reads=[cdg, aext], writes=[b])
                    act(ycv.ap[:, c, 0:N], b.ap[:, 0:N], AF.Identity, bias=pv.ap[:, l, 64 + c:65 + c], scale=1.0, reads=[b, pv], writes=[ycv])
                for c in range(2):
                    tt(f1[c].ap[:, 0:N], ycv.ap[:, c, 0:N], ycv.ap[:, c, 0:N], ALU.mult, reads=[ycv], writes=[f1[c]])
                bm = bank(); bs_ = bank()
                for c in range(2):
                    mm(bm.ap[:, 0:N], C["ones"].ap, ycv.ap[:, c, 0:N], start=(c == 0), stop=(c == 1), reads=[ycv, C["ones"]], writes=[bm])
                for c in range(2):
                    mm(bs_.ap[:, 0:N], C["ones"].ap, f1[c].ap[:, 0:N], start=(c == 0), stop=(c == 1), reads=[f1[c], C["ones"]], writes=[bs_])
                mean = f1[2]; var = f1[3]
                act(mean.ap[:, 0:N], bm.ap[:, 0:N], AF.Identity, scale=1.0 / 256, reads=[bm], writes=[mean])
                tt(var.ap[:, 0:N], mean.ap[:, 0:N], mean.ap[:, 0:N], ALU.mult, reads=[mean], writes=[var])
                stt(var.ap[:, 0:N], bs_.ap[:, 0:N], 1.0 / 256, var.ap[:, 0:N], ALU.mult, ALU.subtract, reads=[bs_, var], writes=[var])
                act(var.ap[:, 0:N], var.ap[:, 0:N], AF.Sqrt, bias=epsc.ap[:, 0:1], scale=1.0, reads=[var, epsc], writes=[var])
                recip(var.ap[:, 0:N], var.ap[:, 0:N], reads=[var], writes=[var])
                for c in range(2):
                    t1 = f1[4]
                    tt(t1.ap[:, 0:N], ycv.ap[:, c, 0:N], mean.ap[:, 0:N], ALU.subtract, reads=[ycv, mean], writes=[t1])
                    stt(t1.ap[:, 0:N], t1.ap[:, 0:N], pv.ap[:, l, 66 + c:67 + c], var.ap[:, 0:N], ALU.mult, ALU.mult, reads=[t1, pv, var], writes=[t1])
                    act(mixb.ap[:, c, 0:N], t1.ap[:, 0:N], AF.Silu, bias=pv.ap[:, l, 68 + c:69 + c], scale=1.0, reads=[t1, pv], writes=[mixb])
                if last:
                    dst = o_conv_p[l] if bq == 0 else o_conv_s[l, sidx]
                    store_tail(dst, 30, aext, 32 + N - 30, 2, True)
            if 'pool' in parts:
                for c in range(2):
                    pp = proj(1544 + c * 128, 128, N)
                    cp(pext.ap[:, c, 16:16 + N], pp.ap[:, 0:N], reads=[pp], writes=[pext])
                W = 16 + N
                for c in range(2):
                    xe = pext.ap[:, c, :]
                    tt(s1.ap[:, 1:W], xe[:, 1:W], xe[:, 0:W - 1], ALU.add, reads=[pext], writes=[s1])
                    tt(s2.ap[:, 3:W], s1.ap[:, 3:W], s1.ap[:, 1:W - 2], ALU.add, reads=[s1], writes=[s2])
                    if c == 1:
                        tt(s3.ap[:, 7:W], s2.ap[:, 7:W], s2.ap[:, 3:W - 4], ALU.add, reads=[s2], writes=[s3])
                        tt(s4.ap[:, 15:W], s3.ap[:, 15:W], s3.ap[:, 7:W - 8], ALU.add, reads=[s3], writes=[s4])
                    lo, hi = (s1, s2) if c == 0 else (s3, s4)
                    pl = b1[1]
                    for (win_, p0) in ((lo, 0), (hi, 64)):
                        stt(pl.ap[p0:p0 + 64, 0:N], win_.ap[p0:p0 + 64, 16:16 + N], C["invw"].ap[p0:p0 + 64, c:c + 1], xe[p0:p0 + 64, 16:16 + N],
                            ALU.mult, ALU.subtract, reads=[win_, C["invw"], pext], writes=[pl])
                        if first and bq == 0:
                            t16 = f1[4]
                            tt(t16.ap[p0:p0 + 64, 0:16], win_.ap[p0:p0 + 64, 16:32], C["invcnt0"].ap[p0:p0 + 64, c, :], ALU.mult, reads=[win_, C["invcnt0"]], writes=[t16])
                            tt(pl.ap[p0:p0 + 64, 0:16], t16.ap[p0:p0 + 64, 0:16], xe[p0:p0 + 64, 16:32], ALU.subtract, reads=[t16, pext], writes=[pl])
                    b = bank()
                    mm(b.ap[:, 0:N], wpb.ap[:, c, :], pl.ap[:, 0:N], reads=[wpb, pl], writes=[b])
                    act(mixb.ap[:, 4 + c, 0:N], b.ap[:, 0:N], AF.Identity, scale=pv.ap[:, l, 70 + c:71 + c], reads=[b, pv], writes=[mixb])
                if last:
                    dst = o_pool_p[l] if bq == 0 else o_pool_s[l, sidx]
                    store_tail(dst, 15, pext, 16 + N - 15, 2, False)
            if 'sb' in parts:
                for i in range(6):
                    pq = proj(1800 + i * 128, 128, N)
                    if i < 4:
                        xs_ = f1[0]
                        act(xs_.ap[:, 0:N], pq.ap[:, 0:N], AF.Copy, reads=[pq], writes=[xs_])
                        if i < 2:
                            headnorm(xs_, N, epsc.ap[:, 1:2], 1.0, mixb.ap[:, 6 + i, 0:N], gcol=pv.ap[:, l, 72:73])
                            mixb.w = ("dve", S.cnt["dve"]); mixb.r = {}
                        else:
                            headnorm(xs_, N, epsc.ap[:, 0:1], 1.0 / 64, ktb.ap[:, i - 2, 0:N], gcol=pv.ap[:, l, 73:74])
                            ktb.w = ("dve", S.cnt["dve"]); ktb.r = {}
                    else:
                        act(vtb.ap[:, i - 4, 0:N], pq.ap[:, 0:N], AF.Copy, reads=[pq], writes=[vtb])
                sbl = (dbg or {}).get('sbl', 9)
                if sbl >= 2:
                    dma((dbg or {}).get('ksq', 'pool'), kT_scr[:, :, col0:col0 + N], ktb.ap[:, :, 0:N], reads=[ktb], writes=[T_kv], key="ks")
                T = min(128, N)
                for t0 in range(0, N if sbl >= 3 else 0, T):
                    b = PSB; bb = bfv(b)
                    for p in range(2):
                        tr(bb[0:T, p * 128:(p + 1) * 128], ktb.ap[:, p, t0:t0 + T], identB, reads=[ktb, CB["ident"]], writes=[b])
                        tr(bb[0:T, 256 + p * 128:256 + (p + 1) * 128], vtb.ap[:, p, t0:t0 + T], identB, reads=[vtb, CB["ident"]], writes=[b])
                    cp(ktok.ap[0:T, :], bb[0:T, 0:256], reads=[b], writes=[ktok])
                    act(vtok.ap[0:T, :], bb[0:T, 256:512], AF.Copy, reads=[b], writes=[vtok])
                    cp(vtokb.ap[0:T, :], bb[0:T, 256:512], reads=[b], writes=[vtokb])
                    if sbl < 4:
                        continue
                    if bq == 0:
                        dk = o_k_p[l, :, col0 + t0:col0 + t0 + T, :]; dv = o_v_p[l, :, col0 + t0:col0 + t0 + T, :]
                    else:
                        dk = o_k_s[l, sidx]; dv = o_v_s[l, sidx]
                    dma("pool", dk.rearrange("h t d -> t h d"), ktok.ap[0:T, :].rearrange("t (h d) -> t h d", h=4), reads=[ktok], writes=[T_out], key="ok")
                    dma("pool", dv.rearrange("h t d -> t h d"), vtok.ap[0:T, :].rearrange("t (h d) -> t h d", h=4), reads=[vtok], writes=[T_out], key="ov")
                    dma("pool", v_scr[col0 + t0:col0 + t0 + T, :], vtokb.ap[0:T, :], reads=[vtokb], writes=[T_kv], key="vs")
            if 'dn' in parts:
                for i in range(6):
                    pq = proj(512 + i * 128, 128, N)
                    cp(dnext.ap[:, i, 4:4 + N], pq.ap[:, 0:N], reads=[pq], writes=[dnext])
                if last:
                    dst = o_dnc_p[l] if bq == 0 else o_dnc_s[l, sidx]
                    store_tail(dst, 4, dnext, 4 + N - 4, 6, True, r0=1)
                for i in range(6):
                    b = bank()
                    for j in range(4):
                        mm(b.ap[:, 0:N], ddg.ap[:, j * 6 + i, :], dnext.ap[:, i, 1 + j:1 + j + N], start=(j == 0), stop=(j == 3), reads=[ddg, dnext], writes=[b])
                    if i < 4:
                        sl_ = f1[1]
                        act(sl_.ap[:, 0:N], b.ap[:, 0:N], AF.Silu, reads=[b], writes=[sl_])
                        if i < 2:
                            headnorm(sl_, N, epsc.ap[:, 1:2], 64.0, dqk.ap[:, i, 0:N])
                        else:
                            headnorm(sl_, N, epsc.ap[:, 0:1], 1.0, dqk.ap[:, i, 0:N])
                        dqk.w = ("dve", S.cnt["dve"]); dqk.r = {}
                    else:
                        act(dqk.ap[:, i, 0:N], b.ap[:, 0:N], AF.Silu, reads=[b], writes=[dqk])
                for t0 in range(0, N, T):
                    dn_tile(l, bq, col0, N, t0, T, locals())
                if last:
                    dst = o_dn_p[l] if bq == 0 else o_dn_s[l, sidx]
                    for h in range(4):
                        h0 = (h % 2) * 64
                        dma("pool", dst[h], S32.ap[h0:h0 + 64, h // 2, h0:h0 + 64], reads=[S32], writes=[T_out], key="so")
            dma("pool", mixq[:, :, col0:col0 + N], mixb.ap[:, :, 0:N], reads=[mixb], writes=[T_mixq[ui]], key="mq")

    def dn_tile(l, bq, col0, N, t0, T, L):
        (hT, win, sm, dqk, dg, XL, XU, EL, EU, ER, Mb, MT, attnT, r32, rbf, wtok, wT, qdT, kupd, vnew, Ghd, glast, otok, osq, sgate,
         ytok, tmpS, S32, Sbf, mixb, identF, identB, kvt) = [L[k] for k in (
            "hT", "win", "sm", "dqk", "dg", "XL", "XU", "EL", "EU", "ER", "Mb", "MT", "attnT", "r32", "rbf", "wtok", "wT", "qdT", "kupd",
            "vnew", "Ghd", "glast", "otok", "osq", "sgate", "ytok", "tmpS", "S32", "Sbf", "mixb", "identF", "identB", "kvt")]
        tc = slice(t0, t0 + T)
        nch = T // 64
        pab = bank()
        for k in range(8):
            mm(pab.ap[0:T, 0:8], hT.ap[:, k, tc], win.ap[:, k, 1536:1544], start=(k == 0), stop=(k == 7), reads=[hT, win], writes=[pab])
        pgt = bank()
        for k in range(8):
            mm(pgt.ap[0:T, 0:256], hT.ap[:, k, tc], win.ap[:, k, 1280:1536], start=(k == 0), stop=(k == 7), reads=[hT, win], writes=[pgt])
        act(sgate.ap[0:T, 0:256], pgt.ap[0:T, 0:256], AF.Silu, reads=[pgt], writes=[sgate])
        beta, g_, gc, gtot, egc, ekd, t4, ssq, rr = [sm[k] for k in ("beta", "g", "gc", "gtot", "egc", "ekd", "t4", "ssq", "rr")]
        act(beta.ap[0:T, 0:4], pab.ap[0:T, 4:8], AF.Sigmoid, reads=[pab], writes=[beta])
        tt(t4.ap[0:T, 0:4], pab.ap[0:T, 0:4], rowb.ap[0:T, l, 256:260], ALU.add, reads=[pab, rowb], writes=[t4])
        act(t4.ap[0:T, 0:4], t4.ap[0:T, 0:4], AF.Exp, reads=[t4], writes=[t4])
        act(t4.ap[0:T, 0:4], t4.ap[0:T, 0:4], AF.Ln, bias=epsc.ap[0:T, 2:3], scale=1.0, reads=[t4, epsc], writes=[t4])
        tt(g_.ap[0:T, 0:4], t4.ap[0:T, 0:4], negA.ap[0:T, l, :], ALU.mult, reads=[t4, negA], writes=[g_])
        b = bank()
        mm(b.ap[0:T, 0:4], C["ublk"].ap[0:T, 0:T], g_.ap[0:T, 0:4], reads=[C["ublk"], g_], writes=[b])
        mm(b.ap[0:T, 8:12], C["blk64"].ap[0:T, 0:T], g_.ap[0:T, 0:4], reads=[C["blk64"], g_], writes=[b])
        cp(gc.ap[0:T, 0:4], b.ap[0:T, 0:4], reads=[b], writes=[gc])
        act(egc.ap[0:T, 0:4], b.ap[0:T, 0:4], AF.Exp, reads=[b], writes=[egc])
        tt(ekd.ap[0:T, 0:4], b.ap[0:T, 8:12], gc.ap[0:T, 0:4], ALU.subtract, reads=[b, gc], writes=[ekd])
        act(ekd.ap[0:T, 0:4], ekd.ap[0:T, 0:4], AF.Exp, reads=[ekd], writes=[ekd])
        bt = PSB; btb = bfv(bt)
        for p in range(2):
            tr(btb[0:T, p * 128:(p + 1) * 128], dqk.ap[:, 2 + p, tc], identB, reads=[dqk, CB["ident"]], writes=[bt])
            tr(btb[0:T, 256 + p * 128:256 + (p + 1) * 128], dqk.ap[:, 4 + p, tc], identB, reads=[dqk, CB["ident"]], writes=[bt])
        cp(kvt.ap[0:T, :], btb[0:T, 0:512], reads=[bt], writes=[kvt])
        btb = kvt.ap; bt = kvt
        bKK = bank(); bQK = bank(); bR = bank()
        for h in range(4):
            p, hs = h // 2, slice((h % 2) * 64, (h % 2) * 64 + 64)
            mm(bKK.ap[0:T, h * 128:h * 128 + T], dqk.ap[hs, 2 + p, tc], dqk.ap[hs, 2 + p, tc], reads=[dqk], writes=[bKK])
            mm(bQK.ap[0:T, h * 128:h * 128 + T], dqk.ap[hs, 2 + p, tc], dqk.ap[hs, p, tc], reads=[dqk], writes=[bQK])
            ts(dg.ap[0:T, h, 0:T], identF[0:T, 0:T], gc.ap[0:T, h:h + 1], ALU.mult, reads=[C["ident"], gc], writes=[dg])
            mm(bR.ap[:, h * 128:h * 128 + T], C["ones"].ap[0:T, :], dg.ap[0:T, h, 0:T], reads=[C["ones"], dg], writes=[bR])
        for h in range(4):
            stt(XL.ap[0:T, h, 0:T], bR.ap[0:T, h * 128:h * 128 + T], gc.ap[0:T, h:h + 1], C["pmL"].ap[0:T, 0:T], ALU.subtract, ALU.add,
                reads=[bR, gc, C["pmL"]], writes=[XL])
            stt(XU.ap[0:T, h, 0:T], bR.ap[0:T, h * 128:h * 128 + T], gc.ap[0:T, h:h + 1], C["nmU"].ap[0:T, 0:T], ALU.subtract, ALU.add,
                reads=[bR, gc, C["nmU"]], writes=[XU])
        act(EL.ap[0:T, :, 0:T], XL.ap[0:T, :, 0:T], AF.Exp, scale=-1.0, reads=[XL], writes=[EL])
        act(EU.ap[0:T, :, 0:T], XU.ap[0:T, :, 0:T], AF.Exp, reads=[XU], writes=[EU])
        M0 = Mb[0]
        for h in range(4):
            stt(M0.ap[0:T, h, 0:T], bKK.ap[0:T, h * 128:h * 128 + T], beta.ap[0:T, h:h + 1], EL.ap[0:T, h, 0:T], ALU.mult, ALU.mult,
                reads=[bKK, beta, EL], writes=[M0])
        tt(attnT.ap[0:T, :, 0:T], bQK.ap[:, :].rearrange("p (h t) -> p h t", h=4)[0:T, :, 0:T], EU.ap[0:T, :, 0:T], ALU.mult,
           reads=[bQK, EU], writes=[attnT])
        act(ER.ap[:, :, 0:T], bR.ap[:, :].rearrange("p (h t) -> p h t", h=4)[:, :, 0:T], AF.Exp, reads=[bR], writes=[ER])
        bL = bank()
        for h in range(4):
            tr(bL.ap[0:T, h * 128:h * 128 + T], M0.ap[0:T, h, 0:T], identF[0:T, 0:T], reads=[M0, C["ident"]], writes=[bL])
        cp(MT[0].ap[0:T, :, 0:T], bL.ap[:, :].rearrange("p (h t) -> p h t", h=4)[0:T, :, 0:T], reads=[bL], writes=[MT[0]])
        Mc = M0
        for lev in range(1, 6):
            bA = bank(); bB = bank()
            Mn = Mb[lev % 2]
            for h in range(4):
                mm(bA.ap[0:T, h * 128:h * 128 + T], MT[lev - 1].ap[0:T, h, 0:T], Mc.ap[0:T, h, 0:T], reads=[MT[lev - 1], Mc], writes=[bA])
                mm(bB.ap[0:T, h * 128:h * 128 + T], Mc.ap[0:T, h, 0:T], MT[lev - 1].ap[0:T, h, 0:T], reads=[MT[lev - 1], Mc], writes=[bB])
            if lev < 5:
                cp(Mn.ap[0:T, :, 0:T], bA.ap[:, :].rearrange("p (h t) -> p h t", h=4)[0:T, :, 0:T], reads=[bA], writes=[Mn])
            act(MT[lev].ap[0:T, :, 0:T], bB.ap[:, :].rearrange("p (h t) -> p h t", h=4)[0:T, :, 0:T], AF.Copy, reads=[bB], writes=[MT[lev]])
            Mc = Mn
        for h in range(4):
            ts(r32.ap[0:T, h, 0:64], btb[0:T, 256 + h * 64:256 + (h + 1) * 64], beta.ap[0:T, h:h + 1], ALU.mult, reads=[bt, beta], writes=[r32])
            ts(r32.ap[0:T, h, 64:128], btb[0:T, h * 64:(h + 1) * 64], beta.ap[0:T, h:h + 1], ALU.mult, egc.ap[0:T, h:h + 1], ALU.mult,
               reads=[bt, beta, egc], writes=[r32])
            ts(kupd.ap[0:T, h // 2, (h % 2) * 64:(h % 2) * 64 + 64], btb[0:T, h * 64:(h + 1) * 64], ekd.ap[0:T, h:h + 1], ALU.mult,
               reads=[bt, ekd], writes=[kupd])
        for lev in range(6):
            bx = bank()
            for h in range(4):
                mm(bx.ap[0:T, h * 128:(h + 1) * 128], MT[lev].ap[0:T, h, 0:T], rbf.ap[0:T, h, :], reads=[MT[lev], rbf], writes=[bx])
            tt(r32.ap[0:T], r32.ap[0:T], bx.ap[:, :].rearrange("p (h t) -> p h t", h=4)[0:T], ALU.subtract if lev == 0 else ALU.add,
               reads=[r32, bx], writes=[r32])
        cp(wtok.ap[0:T].rearrange("t p (h d) -> t (p h) d", h=2), rbf.ap[0:T, :, 64:128], reads=[rbf], writes=[wtok])
        bw = PSB; bwb = bfv(bw)
        for p in range(2):
            tr(bwb[:, p * 128:p * 128 + T], wtok.ap[0:T, p, :], identB[0:T, 0:T], reads=[wtok, CB["ident"]], writes=[bw])
        cp(wT.ap[:, :, 0:T], bwb[:, 0:256].rearrange("p (a t) -> p a t", a=2)[:, :, 0:T], reads=[bw], writes=[wT])
        for h in range(4):
            p, hs = h // 2, slice((h % 2) * 64, (h % 2) * 64 + 64)
            tt(qdT.ap[hs, p, 0:T], dqk.ap[hs, p, tc], ER.ap[hs, h, 0:T], ALU.mult, reads=[dqk, ER], writes=[qdT])
        for h in range(4):
            ts(Ghd.ap[0:T, h // 2, (h % 2) * 64:(h % 2) * 64 + 64], C["ones"].ap[0:T, 0:64], g_.ap[0:T, h:h + 1], ALU.mult,
               reads=[C["ones"], g_], writes=[Ghd])
        bg = bank()
        for p in range(2):
            mm(bg.ap[:, p * 2:p * 2 + 2], Ghd.ap[0:T, p, :], C["chunkind"].ap[0:T, :], reads=[Ghd, C["chunkind"]], writes=[bg])
        act(glast.ap, bg.ap[:, 0:4].rearrange("p (a c) -> p a c", a=2), AF.Exp, reads=[bg], writes=[glast])
        bO = PS_DED
        for ci in range(nch):
            cs = slice(ci * 64, ci * 64 + 64)
            bV = bank()
            for p in range(2):
                mm(bV.ap[cs, p * 128:(p + 1) * 128], wT.ap[:, p, cs], Sbf.ap[:, p, :], reads=[wT, Sbf], writes=[bV])
            tt(vnew.ap[cs, :].rearrange("c (h e) -> c h e", h=4), r32.ap[cs, :, 0:64], bV.ap[cs, 0:256].rearrange("c (h e) -> c h e", h=4),
               ALU.subtract, reads=[r32, bV], writes=[vnew])
            for p in range(2):
                mm(bO.ap[cs, p * 128:(p + 1) * 128], qdT.ap[:, p, cs], Sbf.ap[:, p, :], start=True, stop=False, reads=[qdT, Sbf], writes=[bO])
                for hh in range(2):
                    h = 2 * p + hh
                    mm(bO.ap[cs, h * 64:(h + 1) * 64], attnT.ap[cs, h, cs], vnew.ap[cs, h * 64:(h + 1) * 64], start=False, stop=(hh == 1),
                       reads=[attnT, vnew], writes=[bO])
            for p in range(2):
                bS = bank()
                mm(bS.ap[:, 0:128], kupd.ap[cs, p, :], vnew.ap[cs, p * 128:(p + 1) * 128], reads=[kupd, vnew], writes=[bS])
                tt(tmpS.ap[:, p, :], bS.ap[:, 0:128], C["blk64"].ap, ALU.mult, reads=[bS, C["blk64"]], writes=[tmpS])
                stt(S32.ap[:, p, :], S32.ap[:, p, :], glast.ap[:, p, ci:ci + 1], tmpS.ap[:, p, :], ALU.mult, ALU.add,
                    reads=[S32, glast, tmpS], writes=[S32])
            act(Sbf.ap, S32.ap, AF.Copy, reads=[S32], writes=[Sbf])
        cp(otok.ap[0:T, 0:256], bO.ap[0:T, 0:256], reads=[bO], writes=[otok])
        tt(osq.ap[0:T, 0:256], otok.ap[0:T, 0:256], otok.ap[0:T, 0:256], ALU.mult, reads=[otok], writes=[osq])
        red(ssq.ap[0:T, 0:4], osq.ap[0:T, 0:256].rearrange("t (h e) -> t h e", h=4), reads=[osq], writes=[ssq])
        act(rr.ap[0:T, 0:4], ssq.ap[0:T, 0:4], AF.Sqrt, bias=epsc.ap[0:T, 0:1], scale=1.0 / 64, reads=[ssq, epsc], writes=[rr])
        recip(rr.ap[0:T, 0:4], rr.ap[0:T, 0:4], reads=[rr], writes=[rr])
        for h in range(4):
            ts(osq.ap[0:T, h * 64:(h + 1) * 64], otok.ap[0:T, h * 64:(h + 1) * 64], rr.ap[0:T, h:h + 1], ALU.mult, reads=[otok, rr], writes=[osq])
        tt(osq.ap[0:T, 0:256], osq.ap[0:T, 0:256], rowb.ap[0:T, l, 0:256], ALU.mult, reads=[osq, rowb], writes=[osq])
        tt(ytok.ap[0:T, :], osq.ap[0:T, 0:256], sgate.ap[0:T, 0:256], ALU.mult, reads=[osq, sgate], writes=[ytok])
        by = PSB; byb = bfv(by)
        for p in range(2):
            tr(byb[:, p * 128:p * 128 + T], ytok.ap[0:T, p * 128:(p + 1) * 128], identB[0:T, 0:T], reads=[ytok, CB["ident"]], writes=[by])
        cp(mixb.ap[:, 2:4, tc], byb[:, 0:256].rearrange("p (a t) -> p a t", a=2)[:, :, 0:T], reads=[by], writes=[mixb])

    def layer_A2(l):
        S.barrier()
        A.off = A_persist
        al = A.alloc
        KMAX = max(SEQ, 1152); NB = KMAX // 128
        wout = al([8, D], BF16)
        kT = al([2, KMAX], BF16); vS = al([NB, 256], BF16)
        kst = al([8, 256], F32)
        mixb = al([8, 512], BF16); x = al([8, 512], F32)
        eb = [al([512], F32) for _ in range(2)]; spb = [al([512], BF16) for _ in range(2)]
        wb_ = [al([512], F32) for _ in range(2)]; ab_ = [al([512], BF16) for _ in range(2)]
        acc = al([512], BF16)
        identF = C["ident"].ap
        dma("sp", wout.ap, wb_out[l], reads=[T_w], writes=[wout], key="wo")
        it = 0
        prompt_loaded = False
        for (bq, col0, N, first, last, ui) in units:
            sidx = bq - 1
            if bq > 0:
                for h in range(4):
                    dma("sp", kst.ap[:, :, h * 64:(h + 1) * 64], cache_v[l, sidx, h].rearrange("(n p) d -> p n d", p=128), writes=[kst], key="cv")
                cp(vS.ap[:, 0:8, :], kst.ap, reads=[kst], writes=[vS])
                dma("sp", vS.ap[0:64, 8, :], v_scr[col0:col0 + 64, :], reads=[T_kv], writes=[vS], key="cv2")
                for h in range(4):
                    dma("sp", kst.ap[:, :, h * 64:(h + 1) * 64], cache_k[l, sidx, h].rearrange("(n p) d -> p n d", p=128), writes=[kst], key="ck")
                for p in range(2):
                    for n0 in range(0, 8, 4):
                        b = bank()
                        for n in range(4):
                            tr(b.ap[:, n * 128:(n + 1) * 128], kst.ap[:, n0 + n, p * 128:(p + 1) * 128], identF, reads=[kst, C["ident"]], writes=[b])
                        cp(kT.ap[:, p, n0 * 128:(n0 + 4) * 128], b.ap[:, :], reads=[b], writes=[kT])
                dma("sp", kT.ap[:, :, 1024:1088], kT_scr[:, :, col0:col0 + 64], reads=[T_kv], writes=[kT], key="ck2")
                blocks = [(kb, 128, None) for kb in range(8)] + [(8, 64, 0)]
            else:
                if not prompt_loaded:
                    dma("sp", kT.ap[:, :, 0:SEQ], kT_scr[:, :, 0:SEQ], reads=[T_kv], writes=[kT], key="pk")
                    dma("sp", vS.ap[:, 0:SEQ // 128, :], v_scr[0:SEQ, :].rearrange("(n p) c -> p n c", p=128), reads=[T_kv], writes=[vS], key="pv")
                    prompt_loaded = True
                u = col0 // 512
                blocks = [(kb, 128, (kb - 4 * u) if kb >= 4 * u else None) for kb in range(4 * u + 4)]
            dma("sp", mixb.ap[:, :, 0:N], mixq[:, :, col0:col0 + N], reads=[T_mixq[ui]], writes=[mixb], key="mx")
            dma("sp", x.ap[:, :, 0:N], xT_v[:, :, col0:col0 + N], reads=[T_xT[ui]], writes=[x], key="a2x")
            for p in range(2):
                bo = PS_DED
                for hh in range(2):
                    h = 2 * p + hh
                    hs = slice(hh * 64, hh * 64 + 64)
                    if bq > 0:
                        mset(acc.ap, 0.0, writes=[acc])
                    for bi, (kb, KR, msk) in enumerate(reversed(blocks)):
                        isf = (bi == 0)
                        e_ = eb[it % 2]; sp_ = spb[it % 2]; w_ = wb_[it % 2]; a_ = ab_[it % 2]
                        it += 1
                        bz = bank()
                        mm(bz.ap[0:KR, 0:N], kT.ap[hs, p, kb * 128:kb * 128 + KR], mixb.ap[hs, 6 + p, 0:N], reads=[kT, mixb], writes=[bz])
                        act(e_.ap[0:KR, 0:N], bz.ap[0:KR, 0:N], AF.Exp, reads=[bz], writes=[e_])
                        if msk is not None:
                            tt(e_.ap[0:KR, 0:N], e_.ap[0:KR, 0:N], CB["sbmask"].ap[0:KR, msk, 0:N], ALU.mult, reads=[e_, CB["sbmask"]], writes=[e_])
                        act(sp_.ap[0:KR, 0:N], e_.ap[0:KR, 0:N], AF.Ln, bias=epsc.ap[0:KR, 2:3], scale=1.0, reads=[e_, epsc], writes=[sp_])
                        bc = bank()
                        mm(bc.ap[0:KR, 0:N], CB["upper"].ap[0:KR, 0:KR], sp_.ap[0:KR, 0:N], start=True, stop=isf, reads=[CB["upper"], sp_], writes=[bc])
                        if not isf:
                            mm(bc.ap[0:KR, 0:N], CB["ones"].ap[:, 0:KR], acc.ap[:, 0:N], start=False, stop=True, reads=[CB["ones"], acc], writes=[bc])
                        act(w_.ap[0:KR, 0:N], bc.ap[0:KR, 0:N], AF.Exp, scale=-1.0, reads=[bc], writes=[w_])
                        tt(a_.ap[0:KR, 0:N], e_.ap[0:KR, 0:N], w_.ap[0:KR, 0:N], ALU.mult, reads=[e_, w_], writes=[a_])
                        if isf and bq == 0:
                            cp(acc.ap[0:KR, 0:N], sp_.ap[0:KR, 0:N], reads=[sp_], writes=[acc], eng="pool")
                        else:
                            tt(acc.ap[0:KR, 0:N], acc.ap[0:KR, 0:N], sp_.ap[0:KR, 0:N], ALU.add, reads=[acc, sp_], writes=[acc], eng="pool")
                        mm(bo.ap[hs, 0:N], vS.ap[0:KR, kb, h * 64:(h + 1) * 64], a_.ap[0:KR, 0:N], start=isf, stop=(bi == len(blocks) - 1),
                           reads=[vS, a_], writes=[bo])
                cp(mixb.ap[:, 6 + p, 0:N], bo.ap[:, 0:N], reads=[bo], writes=[mixb])
            for m in range(8):
                po = bank()
                for k in range(8):
                    mm(po.ap[:, 0:N], wout.ap[:, k, m * 128:(m + 1) * 128], mixb.ap[:, k, 0:N], start=(k == 0), stop=(k == 7), reads=[wout, mixb], writes=[po])
                stt(x.ap[:, m, 0:N], po.ap[:, 0:N], modc(l, 2, m, bq), x.ap[:, m, 0:N], ALU.mult, ALU.add, reads=[po, mod, x], writes=[x])
            dma("pool", xT_v[:, :, col0:col0 + N], x.ap[:, :, 0:N], reads=[x], writes=[T_xT[ui]], key="a2o")

    for l in range(DEPTH):
        if dbg is not None and dbg.get("skip_layers"):
            break
        only = (dbg or {}).get("only", "A1A2B")
        if "A1" in only:
            layer_A1(l)
        if "A2" in only:
            layer_A2(l)
        if "B" in only:
            layer_B(l)

    S.barrier()
    A.off = A_persist
    xin = [A.alloc([8, 128], F32) for _ in range(2)]
    xo = [A.alloc([1024], F32) for _ in range(2)]
    for t in range(ntile):
        dst = yp[t * 128:(t + 1) * 128, :] if t < SEQ // 128 else ys[(t - SEQ // 128) * 128:(t - SEQ // 128 + 1) * 128, :]
        xi = xin[t % 2]; xq = xo[t % 2]
        dma("sp", xi.ap, xT_v[:, :, t * 128:(t + 1) * 128], reads=T_xT, writes=[xi], key="yi%d" % (t % 2))
        for hlf in range(2):
            b = bank()
            for k in range(4):
                tr(b.ap[:, k * 128:(k + 1) * 128], xi.ap[:, hlf * 4 + k, :], C["ident"].ap, reads=[xi, C["ident"]], writes=[b])
            if hlf == 0:
                cp(xq.ap[:, 0:512], b.ap[:, :], reads=[b], writes=[xq])
            else:
                act(xq.ap[:, 512:1024], b.ap[:, :], AF.Copy, reads=[b], writes=[xq])
        dma("pool", dst, xq.ap, reads=[xq], writes=[T_out], key="yo%d" % (t % 2))
    S.barrier()
    S.emit()
    es.close()
    return nc


def host_layout(inp, DEPTH):
    pvrows = np.zeros((DEPTH, NPV, 128), np.float32)
    rowb = np.zeros((DEPTH, 128, 264), np.float32)
    for l in range(DEPTH):
        r = pvrows[l]
        r[0:48] = inp["b_ada"][l].reshape(48, 128)
        r[48:56] = inp["norm_mix"][l].reshape(8, 128)
        r[56:64] = inp["norm_ffn"][l].reshape(8, 128)
        r[64:66] = inp["conv_dw_b"][l].reshape(2, 128)
        r[66:68] = inp["conv_ln_g"][l].reshape(2, 128)
        r[68:70] = inp["conv_ln_b"][l].reshape(2, 128)
        r[70:72] = inp["pool_scale"][l].reshape(2, 128)
        r[72] = np.tile(inp["sb_q_norm"][l], 2)
        r[73] = np.tile(inp["sb_k_norm"][l], 2)
        r[74:136] = inp["conv_dw_w"][l].reshape(62, 128)
        r[136:160] = inp["dn_conv_w"][l].reshape(24, 128)
        rowb[l, :, 0:256] = np.tile(inp["dn_norm_g"][l], 4)[None, :]
        rowb[l, :, 256:260] = inp["dn_dt_bias"][l][None, :]
        rowb[l, :, 260:264] = inp["dn_a_log"][l][None, :]
    return pvrows, rowb


_NC_CACHE = {}


def run(inp, SEQ, DEPTH, ncores, dbg=None):
    inp = {k: np.ascontiguousarray(np.asarray(v, dtype=np.float32)) for k, v in inp.items()}
    key = (SEQ, DEPTH, str(dbg))
    if key not in _NC_CACHE:
        _NC_CACHE[key] = build(SEQ, DEPTH, dbg)
    nc = _NC_CACHE[key]
    pvrows, rowb = host_layout(inp, DEPTH)
    consts = host_consts()
    maps = []
    for i in range(ncores):
        sl = slice(4 * i, 4 * i + 4)
        m = {
            "xp": inp["x_prompt"][i], "xs": inp["x_sample"][sl].reshape(256, D),
            "c5": np.concatenate([inp["c_prompt"][i:i + 1], inp["c_sample"][sl]], 0),
            "cache_conv": inp["cache_conv"][:, sl], "state_dn": inp["state_dn"][:, sl],
            "cache_dn_conv": inp["cache_dn_conv"][:, sl], "cache_pool": inp["cache_pool"][:, sl],
            "cache_sb_k": inp["cache_sb_k"][:, sl], "cache_sb_v": inp["cache_sb_v"][:, sl],
            "w_ada": inp["w_ada"], "w_in": inp["w_in"], "w_out": inp["w_out"],
            "ffn_w_gate": inp["ffn_w_gate"], "ffn_w_up": inp["ffn_w_up"], "ffn_w_down": inp["ffn_w_down"],
            "pool_w": inp["pool_w"], "pvrows": pvrows, "rowb": rowb,
        }
        for k, v in consts.items():
            m["c_" + k] = v
        maps.append({k: np.ascontiguousarray(v) for k, v in m.items()})
    res = run_bass_kernel_spmd(nc, maps, core_ids=list(range(ncores)))
    R = res.results
    cat = lambda name, ax: np.concatenate([np.expand_dims(r[name], ax) if name.endswith("_p") or name == "yp" else r[name] for r in R], ax)
    y_p = np.stack([r["yp"] for r in R], 0)
    y_s = np.concatenate([r["ys"].reshape(4, 64, D) for r in R], 0)
    outs = [y_p, y_s]
    for nm in ("conv", "dn", "dnconv", "pool", "sbk", "sbv"):
        outs.append(np.stack([r[nm + "_p"] for r in R], 1))
        outs.append(np.concatenate([r[nm + "_s"] for r in R], 1))
    return tuple(np.ascontiguousarray(o.astype(np.float32)) for o in outs)


def kernel(**inputs):
    return run(inputs, 8192, 4, 8)
```

```python
import contextlib
import numpy as np
import concourse.bass as bass
import concourse.mybir as mybir
from concourse.bass_utils import run_bass_kernel_spmd

F32 = mybir.dt.float32
BF16 = mybir.dt.bfloat16
U8 = mybir.dt.uint8
AF = mybir.ActivationFunctionType
ALU = mybir.AluOpType
AX = mybir.AxisListType

D = 1024
DIN = 2568
DFF = 2816
NJ = 22
EPS = 1e-6
PAST = 1024
NPV = 160
SAME_SYNC = True
SAME_DIST = 6
BIG = 30000.0
INS_TAGS = {}
STOREQ = "sp"


class Tk:
    __slots__ = ("w", "r", "ap", "excl")

    def __init__(self, ap=None, excl=False):
        self.w = None
        self.r = {}
        self.ap = ap
        self.excl = excl

    def __getitem__(self, k):
        return self.ap[k]


class Sched:
    ENG = ("pe", "act", "dve", "pool", "sp")

    def __init__(self, nc, es):
        self.nc = nc
        self.es = es
        self.ops = {e: [] for e in self.ENG}
        self.cnt = {e: 0 for e in self.ENG}
        self.seen = {e: {} for e in self.ENG}
        self.sems = {}
        self.dcnt = {}

    def sem(self, key):
        if key not in self.sems:
            self.sems[key] = self.es.enter_context(self.nc.semaphore("s_" + str(key)))
        return self.sems[key]

    def _deps(self, eng, reads, writes):
        need = {}

        def add(kv, same_ok=True):
            if kv is None:
                return
            k, v = kv
            if k == eng and (eng == "pe" or not SAME_SYNC or not same_ok):
                return
            if k == eng and self.cnt[eng] + 1 - v > SAME_DIST:
                return
            if need.get(k, 0) < v:
                need[k] = v

        for t in reads:
            add(t.w)
        for t in writes:
            add(t.w)
            for k, v in t.r.items():
                add((k, v), same_ok=False)
        waits = []
        for k, v in need.items():
            if self.seen[eng].get(k, 0) >= v:
                continue
            self.seen[eng][k] = v
            waits.append((k, v))
        return waits

    def op(self, eng, fn, reads=(), writes=()):
        import sys as _s
        tag = (_s._getframe(2).f_lineno, _s._getframe(3).f_lineno)
        fn = self._wrap(fn, tag)
        if eng != "pe":
            ex = [t for t in reads if t.excl]
            if ex:
                reads = [t for t in reads if not t.excl]
                writes = list(writes) + ex
        waits = self._deps(eng, reads, writes)
        self.cnt[eng] += 1
        c = self.cnt[eng]
        for t in reads:
            t.r[eng] = c
        for t in writes:
            t.w = (eng, c)
            t.r = {}
        self.ops[eng].append((waits, fn, (eng, 1)))

    def _wrap(self, fn, tag):
        def f(e):
            ins = fn(e)
            try:
                INS_TAGS[str(getattr(ins, "name", None) or getattr(getattr(ins, "ins", None), "name", None))] = tag
            except Exception:
                pass
            return ins
        return f

    def dma(self, q, key, out, in_, reads=(), writes=()):
        waits = self._deps(q, reads, writes)
        v = self.dcnt.get(key, 0) + 16
        self.dcnt[key] = v
        for t in reads:
            t.r[key] = v
        for t in writes:
            t.w = (key, v)
            t.r = {}
        self.ops[q].append((waits, (lambda e: e.dma_start(out=out, in_=in_)), (key, 16)))

    def barrier(self):
        for e in self.ENG:
            waits = []
            for k in self.ENG:
                if k != e and self.cnt[k] > self.seen[e].get(k, 0):
                    waits.append((k, self.cnt[k]))
                    self.seen[e][k] = self.cnt[k]
            for k, v in self.dcnt.items():
                if v > self.seen[e].get(k, 0):
                    waits.append((k, v))
                    self.seen[e][k] = v
            self.ops[e].append((waits, None, None))

    def emit(self):
        nc = self.nc
        for k in list(self.ENG) + list(self.dcnt.keys()):
            self.sem(k)
        with nc.Block() as block:
            def mk(name):
                def body(e):
                    for waits, fn, inc in self.ops[name]:
                        for k, v in waits:
                            e.wait_ge(self.sems[k], v)
                        if fn is not None:
                            fn(e).then_inc(self.sems[inc[0]], inc[1])
                return body
            block.tensor(mk("pe"))
            block.scalar(mk("act"))
            block.vector(mk("dve"))
            block.gpsimd(mk("pool"))
            block.sync(mk("sp"))


class Arena:
    def __init__(self, nc, es, nbytes):
        self.t = es.enter_context(nc.sbuf_tensor("arena", [128, nbytes], U8))
        self.off = 0
        self.nbytes = nbytes

    def alloc(self, shape, dt):
        n = int(np.prod(shape))
        sz = n * (4 if dt == F32 else 2)
        ap = self.t[:, self.off:self.off + sz].bitcast(dt)
        self.off += (sz + 63) // 64 * 64
        assert self.off <= self.nbytes, ("arena overflow", self.off)
        if len(shape) == 2:
            ap = ap.rearrange("p (a b) -> p a b", a=shape[0])
        elif len(shape) == 3:
            ap = ap.rearrange("p (a b c) -> p a b c", a=shape[0], b=shape[1])
        return Tk(ap)


def host_consts():
    p = np.arange(128)
    c = {}
    c["ident"] = np.eye(128, dtype=np.float32)
    c["ones"] = np.ones((128, 128), np.float32)
    same = (p[:, None] // 64) == (p[None, :] // 64)
    c["ublk"] = (same & (p[:, None] <= p[None, :])).astype(np.float32)
    c["blk64"] = same.astype(np.float32)
    c["pmL"] = np.where(same & (p[:, None] > p[None, :]), 0.0, BIG).astype(np.float32)
    c["nmU"] = np.where(same & (p[None, :] >= p[:, None]), 0.0, -BIG).astype(np.float32)
    c["upper"] = (p[:, None] >= p[None, :]).astype(np.float32)
    f = np.arange(512)
    c["sbmask"] = np.stack([((m * 128 + p[:, None]) < f[None, :]).astype(np.float32) for m in range(4)], 1)
    c["chunkind"] = np.stack([(p // 64 == 0), (p // 64 == 1)], 1).astype(np.float32)
    wins = np.array([[2, 4], [8, 16]])
    wpc = wins[:, p // 64].T.astype(np.float32)
    c["invw"] = (1.0 / wpc).astype(np.float32)
    t = np.arange(16)
    c["invcnt0"] = (1.0 / np.minimum(t[None, None, :] + 1, wpc[:, :, None])).astype(np.float32)
    return c


CONST_SHAPES = dict(ident=[128, 128], ones=[128, 128], ublk=[128, 128], blk64=[128, 128], pmL=[128, 128],
                    nmU=[128, 128], upper=[128, 128], sbmask=[128, 4, 512], chunkind=[128, 2],
                    invw=[128, 2], invcnt0=[128, 2, 16])


def build(SEQ, DEPTH, dbg=None):
    NU = SEQ // 512
    NTOK = SEQ + 256
    nc = bass.Bass("TRN2", target_bir_lowering=False)
    es = contextlib.ExitStack()

    def din(name, shape, dt=F32):
        return nc.dram_tensor(name, list(shape), dt, kind="ExternalInput").ap()

    def dout(name, shape):
        return nc.dram_tensor(name, list(shape), F32, kind="ExternalOutput").ap()

    def dscr(name, shape, dt):
        return nc.dram_tensor(name, list(shape), dt, kind="Internal").ap()

    xp = din("xp", [SEQ, D]); xs = din("xs", [256, D]); c5 = din("c5", [5, D])
    cache_conv = din("cache_conv", [DEPTH, 4, 30, 256]); state_dn = din("state_dn", [DEPTH, 4, 4, 64, 64])
    cache_dnc = din("cache_dn_conv", [DEPTH, 4, 3, 768]); cache_pool = din("cache_pool", [DEPTH, 4, 15, 256])
    cache_k = din("cache_sb_k", [DEPTH, 4, 4, PAST, 64]); cache_v = din("cache_sb_v", [DEPTH, 4, 4, PAST, 64])
    w_ada = din("w_ada", [DEPTH, D, 6 * D]); w_in = din("w_in", [DEPTH, D, DIN]); w_out = din("w_out", [DEPTH, D, D])
    w_g = din("ffn_w_gate", [DEPTH, D, DFF]); w_u = din("ffn_w_up", [DEPTH, D, DFF]); w_d = din("ffn_w_down", [DEPTH, DFF, D])
    pool_w = din("pool_w", [DEPTH, 4, 64, 64])
    pvrows = din("pvrows", [DEPTH, NPV, 128]); rowb_d = din("rowb", [DEPTH, 128, 264])
    cd = {k: din("c_" + k, v) for k, v in CONST_SHAPES.items()}

    yp = dout("yp", [SEQ, D]); ys = dout("ys", [256, D])
    o_conv_p = dout("conv_p", [DEPTH, 30, 256]); o_conv_s = dout("conv_s", [DEPTH, 4, 30, 256])
    o_dn_p = dout("dn_p", [DEPTH, 4, 64, 64]); o_dn_s = dout("dn_s", [DEPTH, 4, 4, 64, 64])
    o_dnc_p = dout("dnconv_p", [DEPTH, 3, 768]); o_dnc_s = dout("dnconv_s", [DEPTH, 4, 3, 768])
    o_pool_p = dout("pool_p", [DEPTH, 15, 256]); o_pool_s = dout("pool_s", [DEPTH, 4, 15, 256])
    o_k_p = dout("sbk_p", [DEPTH, 4, SEQ, 64]); o_k_s = dout("sbk_s", [DEPTH, 4, 4, 64, 64])
    o_v_p = dout("sbv_p", [DEPTH, 4, SEQ, 64]); o_v_s = dout("sbv_s", [DEPTH, 4, 4, 64, 64])

    xT = dscr("xT", [D, NTOK], F32)
    wb_in = dscr("wb_in", [DEPTH, 128, 8, DIN], BF16); wb_out = dscr("wb_out", [DEPTH, 128, 8, D], BF16)
    wb_g = dscr("wb_g", [DEPTH, NJ, 128, 8, 128], BF16); wb_u = dscr("wb_u", [DEPTH, NJ, 128, 8, 128], BF16)
    wb_d = dscr("wb_d", [DEPTH, NJ, 128, D], BF16); wb_ada = dscr("wb_ada", [DEPTH, 48, 128, 8, 128], BF16)
    mixq = dscr("mixq", [128, 8, NTOK], BF16)
    kT_scr = dscr("kT_scr", [128, 2, NTOK], BF16)
    v_scr = dscr("v_scr", [NTOK, 256], BF16)
    xT_v = xT.rearrange("(k p) t -> p k t", p=128)

    S = Sched(nc, es)
    A = Arena(nc, es, 212000)
    PDt = [es.enter_context(nc.psum_tensor("pd%d" % i, [128, 1024], F32)) for i in range(2)]
    PD = [Tk(t[:, :], excl=True) for t in PDt]
    PH = [Tk(t[:, h * 512:(h + 1) * 512], excl=True) for t in PDt for h in range(2)]
    PSs = [Tk(es.enter_context(nc.psum_tensor("ps%d" % i, [128, 512], F32)), excl=True) for i in range(3)]
    PSB = Tk(es.enter_context(nc.psum_tensor("psb", [128, 1024], BF16)), excl=True)
    PS_DED = PSs[2]
    ROT = [PH[0], PH[1], PH[2], PH[3], PSs[0], PSs[1]]
    psi = [0]

    def bank():
        b = ROT[psi[0] % len(ROT)]
        psi[0] += 1
        return b

    def bfv(b):
        return b.ap

    def mm(out, lhsT, rhs, start=True, stop=True, reads=(), writes=()):
        S.op("pe", lambda e: e.matmul(out, lhsT=lhsT, rhs=rhs, start=start, stop=stop), reads, writes)

    def tr(out, in_, ident, reads=(), writes=()):
        S.op("pe", lambda e: e.transpose(out, in_, ident), reads, writes)

    def act(out, in_, func, bias=None, scale=None, reads=(), writes=(), eng="act"):
        kw = {}
        if bias is not None:
            kw["bias"] = bias
        if scale is not None:
            kw["scale"] = scale
        S.op("act", lambda e: e.activation(out=out, in_=in_, func=func, **kw), reads, writes)

    def tt(out, in0, in1, op, reads=(), writes=(), eng="dve"):
        S.op(eng, lambda e: e.tensor_tensor(out=out, in0=in0, in1=in1, op=op), reads, writes)

    def ts(out, in0, s1, op0, s2=None, op1=None, reads=(), writes=(), eng="dve"):
        if op1 is None:
            S.op(eng, lambda e: e.tensor_scalar(out=out, in0=in0, scalar1=s1, scalar2=None, op0=op0), reads, writes)
        else:
            S.op(eng, lambda e: e.tensor_scalar(out=out, in0=in0, scalar1=s1, scalar2=s2, op0=op0, op1=op1), reads, writes)

    def stt(out, in0, scalar, in1, op0, op1, reads=(), writes=()):
        S.op("dve", lambda e: e.scalar_tensor_tensor(out=out, in0=in0, scalar=scalar, in1=in1, op0=op0, op1=op1), reads, writes)

    def cp(out, in_, reads=(), writes=(), eng="dve"):
        S.op(eng, lambda e: e.tensor_copy(out=out, in_=in_), reads, writes)

    def recip(out, in_, reads=(), writes=()):
        S.op("dve", lambda e: e.reciprocal(out=out, in_=in_), reads, writes)

    def mset(ap, val, writes=(), eng="pool"):
        S.op(eng, lambda e: e.memset(ap, val), (), writes)

    def red(out, in_, reads=(), writes=()):
        S.op("dve", lambda e: e.tensor_reduce(out=out, in_=in_, axis=AX.X, op=ALU.add), reads, writes)

    dkeys = [0]

    def dma(q, out, in_, reads=(), writes=(), key=None):
        q = STOREQ if q == "pool" else q
        if key is None:
            key = "d%d" % (dkeys[0] % 24)
            dkeys[0] += 1
        S.dma(q, key, out, in_, reads, writes)

    T_xT = [Tk() for _ in range(NU + 4)]
    T_w = Tk(); T_mixq = [Tk() for _ in range(NU + 4)]; T_kv = Tk(); T_out = Tk()

    C = {}
    for k, shp in CONST_SHAPES.items():
        if k != "sbmask":
            C[k] = A.alloc(shp[1:], F32)
    CB = {k: A.alloc(CONST_SHAPES[k][1:], BF16) for k in ("ident", "ones", "blk64", "upper", "sbmask")}
    pv = A.alloc([DEPTH, NPV], F32)
    mod = A.alloc([DEPTH * 48, 5], F32)
    gs = A.alloc([DEPTH * 16, 5], F32)
    rowb = A.alloc([DEPTH, 264], F32)
    negA = A.alloc([DEPTH, 4], F32)
    scT = A.alloc([8, 5], BF16)
    A_persist = A.off
    C["sbmask"] = A.alloc(CONST_SHAPES["sbmask"][1:], F32)

    for k in CONST_SHAPES:
        dma("sp", C[k].ap, cd[k], writes=[C[k]])
    for k in CB:
        cp(CB[k].ap, C[k].ap, reads=[C[k]], writes=[CB[k]])
    for l in range(DEPTH):
        dma("sp", rowb.ap[:, l, :], rowb_d[l], writes=[rowb])
    act(negA.ap, rowb.ap[:, :, 260:264], AF.Exp, reads=[rowb], writes=[negA])
    ts(negA.ap, negA.ap, -1.0, ALU.mult, reads=[negA], writes=[negA])

    st_a = A.alloc([NPV, 128], F32)
    for l in range(DEPTH):
        for (r0, nr) in ((0, 128), (128, NPV - 128)):
            dma("sp", st_a.ap[0:nr, 0, :], pvrows[l, r0:r0 + nr, :], writes=[st_a])
            b = bank()
            tr(b.ap[:, 0:nr], st_a.ap[0:nr, 0, :], C["ident"].ap[0:nr, 0:nr], reads=[st_a, C["ident"]], writes=[b])
            cp(pv.ap[:, l, r0:r0 + nr], b.ap[:, 0:nr], reads=[b], writes=[pv])
    dma("sp", st_a.ap[0:5, 0:8, :], c5.rearrange("b (k f) -> b k f", k=8), writes=[st_a])
    act(st_a.ap[0:5, 8:16, :], st_a.ap[0:5, 0:8, :], AF.Silu, reads=[st_a], writes=[st_a])
    b = bank()
    for k in range(8):
        tr(b.ap[:, k * 8:k * 8 + 5], st_a.ap[0:5, 8 + k, :], C["ident"].ap[0:5, 0:5], reads=[st_a, C["ident"]], writes=[b])
    cp(scT.ap, b.ap[:, 0:64].rearrange("p (k e) -> p k e", e=8)[:, :, 0:5], reads=[b], writes=[scT])
    S.barrier()
    A.off = A_persist

    stg = [A.alloc([4096], F32) for _ in range(3)]
    stb = [A.alloc([4096], BF16) for _ in range(3)]
    wi = [0]

    def wcast(dst, src, n):
        i = wi[0] % 3
        wi[0] += 1
        sh = src.shape
        sview = stg[i].ap[:, 0:n]
        bview = stb[i].ap[:, 0:n]
        if len(sh) == 3:
            sview = sview.rearrange("p (a b) -> p a b", a=sh[1]); bview = bview.rearrange("p (a b) -> p a b", a=sh[1])
        elif len(sh) == 4:
            sview = sview.rearrange("p (a b c) -> p a b c", a=sh[1], b=sh[2]); bview = bview.rearrange("p (a b c) -> p a b c", a=sh[1], b=sh[2])
        dma("sp", sview, src, writes=[stg[i]], key="wl%d" % i)
        eng = ("act", "dve", "pool")[i]
        if eng == "act":
            act(stb[i].ap[:, 0:n], stg[i].ap[:, 0:n], AF.Copy, reads=[stg[i]], writes=[stb[i]])
        else:
            cp(stb[i].ap[:, 0:n], stg[i].ap[:, 0:n], reads=[stg[i]], writes=[stb[i]], eng=eng)
        dma("pool", dst, bview, reads=[stb[i]], writes=[T_w], key="ws%d" % i)

    for l in range(DEPTH):
        for k in range(8):
            wcast(wb_in[l, :, k, :], w_in[l, k * 128:(k + 1) * 128, :], DIN)
        for k in range(0, 8, 4):
            wcast(wb_out[l, :, k:k + 4, :], w_out[l].rearrange("(k p) c -> p k c", p=128)[:, k:k + 4, :], 4096)
        for j in range(NJ):
            wcast(wb_g[l, j], w_g[l].rearrange("(k p) (j c) -> p j k c", p=128, c=128)[:, j], 1024)
            wcast(wb_u[l, j], w_u[l].rearrange("(k p) (j c) -> p j k c", p=128, c=128)[:, j], 1024)
        for j in range(0, NJ, 2):
            wcast(wb_d[l, j:j + 2].rearrange("j p c -> p j c"), w_d[l].rearrange("(j p) c -> p j c", p=128)[:, j:j + 2, :], 2048)
        for j in range(48):
            wcast(wb_ada[l, j], w_ada[l].rearrange("(k p) (j c) -> p j k c", p=128, c=128)[:, j], 1024)
    S.barrier()
    A.off = A_persist

    wa = [A.alloc([8, 128], BF16) for _ in range(3)]
    for l in range(DEPTH):
        for j in range(48):
            w = wa[j % 3]
            dma("sp", w.ap, wb_ada[l, j], reads=[T_w], writes=[w], key="wa%d" % (j % 3))
            b = bank()
            for k in range(8):
                mm(b.ap[:, 0:5], w.ap[:, k, :], scT.ap[:, k, :], start=(k == 0), stop=(k == 7), reads=[w, scT], writes=[b])
            ts(mod.ap[:, l * 48 + j, :], b.ap[:, 0:5], pv.ap[:, l, j:j + 1], ALU.add, reads=[b, pv], writes=[mod])
        for sub in range(2):
            for k in range(8):
                ts(gs.ap[:, l * 16 + sub * 8 + k, :], mod.ap[:, l * 48 + (1 + 3 * sub) * 8 + k, :], 1.0, ALU.add,
                   pv.ap[:, l, 48 + sub * 8 + k:49 + sub * 8 + k], ALU.mult, reads=[mod, pv], writes=[gs])

    def modc(l, m, k, bq):
        return mod.ap[:, l * 48 + m * 8 + k, bq:bq + 1]

    xin = [A.alloc([1024], F32) for _ in range(2)]
    xo = [A.alloc([8, 128], F32) for _ in range(2)]
    ntile = NTOK // 128
    for t in range(ntile):
        src = xp[t * 128:(t + 1) * 128, :] if t < SEQ // 128 else xs[(t - SEQ // 128) * 128:(t - SEQ // 128 + 1) * 128, :]
        xi = xin[t % 2]; xq = xo[t % 2]
        dma("sp", xi.ap, src, writes=[xi], key="xi%d" % (t % 2))
        for hlf in range(2):
            b = bank()
            for k in range(4):
                tr(b.ap[:, k * 128:(k + 1) * 128], xi.ap[:, (hlf * 4 + k) * 128:(hlf * 4 + k + 1) * 128], C["ident"].ap,
                   reads=[xi, C["ident"]], writes=[b])
            if hlf == 0:
                cp(xq.ap[:, 0:4, :], b.ap[:, :].rearrange("p (k c) -> p k c", k=4), reads=[b], writes=[xq])
            else:
                act(xq.ap[:, 4:8, :], b.ap[:, :].rearrange("p (k c) -> p k c", k=4), AF.Copy, reads=[b], writes=[xq])
        dma("pool", xT_v[:, :, t * 128:(t + 1) * 128], xq.ap, reads=[xq], writes=T_xT, key="xo%d" % (t % 2))
    S.barrier()
    A.off = A_persist

    units = [(s + 1, SEQ + s * 64, 64, True, True, NU + s) for s in range(4)]
    units += [(0, u * 512, 512, u == 0, u == NU - 1, u) for u in range(NU)]

    epsc = A.alloc([4], F32)
    mset(epsc.ap[:, 0:1], EPS, writes=[epsc])
    mset(epsc.ap[:, 1:2], 64 * EPS, writes=[epsc])
    mset(epsc.ap[:, 2:3], 1.0, writes=[epsc])
    A_persist = A.off

    def rms_h(l, sub, bq, x, N, hT, x2, rs, tmps):
        b = bank()
        for k in range(8):
            xq_ = x2[k % 2]
            act(xq_.ap[:, 0:N], x.ap[:, k, 0:N], AF.Square, reads=[x], writes=[xq_])
            mm(b.ap[:, 0:N], CB["ones"].ap, xq_.ap[:, 0:N], start=(k == 0), stop=(k == 7), reads=[xq_, CB["ones"]], writes=[b])
        act(rs.ap[:, 0:N], b.ap[:, 0:N], AF.Sqrt, bias=epsc.ap[:, 0:1], scale=1.0 / D, reads=[b, epsc], writes=[rs])
        recip(rs.ap[:, 0:N], rs.ap[:, 0:N], reads=[rs], writes=[rs])
        for k in range(8):
            t = tmps[k % 2]
            stt(t.ap[:, 0:N], x.ap[:, k, 0:N], gs.ap[:, l * 16 + sub * 8 + k, bq:bq + 1], rs.ap[:, 0:N], ALU.mult, ALU.mult,
                reads=[x, gs, rs], writes=[t])
            act(hT.ap[:, k, 0:N], t.ap[:, 0:N], AF.Identity, bias=modc(l, 3 * sub, k, bq), scale=1.0, reads=[t, mod], writes=[hT])

    def layer_B(l):
        S.barrier()
        A.off = A_persist
        ROT[:] = [PH[0], PH[1], PH[2], PH[3], PSs[0], PSs[1]]
        G = 2
        xg = [A.alloc([8, 512], F32) for _ in range(G)]
        hT = [A.alloc([8, 512], BF16) for _ in range(G)]
        hid = [A.alloc([NJ, 512], BF16) for _ in range(G)]
        wd = A.alloc([NJ, D], BF16)
        wgb = [A.alloc([8, 128], BF16) for _ in range(3)]
        wub = [A.alloc([8, 128], BF16) for _ in range(3)]
        x2 = [A.alloc([512], BF16) for _ in range(2)]
        rs = A.alloc([512], F32)
        tmps = [A.alloc([512], F32) for _ in range(2)]
        sg = [A.alloc([512], F32) for _ in range(2)]
        dma("sp", wd.ap, wb_d[l].rearrange("j p c -> p j c"), reads=[T_w], writes=[wd], key="wd")
        groups = [units[i:i + G] for i in range(0, len(units), G)]
        wc = 0
        for grp in groups:
            for i, (bq, col0, N, first, last, ui) in enumerate(grp):
                dma("sp", xg[i].ap[:, :, 0:N], xT_v[:, :, col0:col0 + N], reads=[T_xT[ui]], writes=[xg[i]], key="bx%d" % i)
                rms_h(l, 1, bq, xg[i], N, hT[i], x2, rs, tmps)
            for j in range(NJ):
                wg_ = wgb[wc % 3]; wu_ = wub[wc % 3]
                dma("sp", wg_.ap, wb_g[l, j], reads=[T_w], writes=[wg_], key="wg%d" % (wc % 3))
                dma("sp", wu_.ap, wb_u[l, j], reads=[T_w], writes=[wu_], key="wu%d" % (wc % 3))
                wc += 1
                for i, (bq, col0, N, first, last, ui) in enumerate(grp):
                    pg = bank(); pu = bank()
                    for k in range(8):
                        mm(pg.ap[:, 0:N], wg_.ap[:, k, :], hT[i].ap[:, k, 0:N], start=(k == 0), stop=(k == 7), reads=[wg_, hT[i]], writes=[pg])
                    for k in range(8):
                        mm(pu.ap[:, 0:N], wu_.ap[:, k, :], hT[i].ap[:, k, 0:N], start=(k == 0), stop=(k == 7), reads=[wu_, hT[i]], writes=[pu])
                    s_ = sg[(j * G + i) % 2]
                    act(s_.ap[:, 0:N], pg.ap[:, 0:N], AF.Silu, reads=[pg], writes=[s_])
                    tt(hid[i].ap[:, j, 0:N], s_.ap[:, 0:N], pu.ap[:, 0:N], ALU.mult, reads=[s_, pu], writes=[hid[i]])
            for i, (bq, col0, N, first, last, ui) in enumerate(grp):
                for m in range(8):
                    po = bank()
                    for j in range(NJ):
                        mm(po.ap[:, 0:N], wd.ap[:, j, m * 128:(m + 1) * 128], hid[i].ap[:, j, 0:N], start=(j == 0), stop=(j == NJ - 1),
                           reads=[wd, hid[i]], writes=[po])
                    stt(xg[i].ap[:, m, 0:N], po.ap[:, 0:N], modc(l, 5, m, bq), xg[i].ap[:, m, 0:N], ALU.mult, ALU.add,
                        reads=[po, mod, xg[i]], writes=[xg[i]])
                dma("pool", xT_v[:, :, col0:col0 + N], xg[i].ap[:, :, 0:N], reads=[xg[i]], writes=[T_xT[ui]], key="bo%d" % i)

    def layer_A1(l):
        S.barrier()
        A.off = A_persist
        ROT[:] = [PH[0], PH[1], PH[2], PH[3], PSs[0], PSs[1]]
        al = A.alloc
        win = al([8, DIN], BF16)
        cdg = al([62, 128], BF16); ddg = al([24, 128], BF16)
        wpf = al([2, 128], F32); wpb = al([2, 128], BF16)
        x = al([8, 512], F32); hT = al([8, 512], BF16)
        x2 = [al([512], BF16) for _ in range(2)]
        rs = al([512], F32); tmps = [al([512], F32) for _ in range(2)]
        mixb = al([8, 512], BF16)
        aext = al([2, 544], BF16); pext = al([2, 528], F32); dnext = al([6, 516], BF16)
        S32 = al([2, 128], F32); Sbf = al([2, 128], BF16)
        stg = al([768], F32)
        f1 = [al([512], F32) for _ in range(6)]
        b1 = [al([512], BF16) for _ in range(3)]
        ycv = al([2, 512], F32)
        dqk = al([6, 512], BF16)
        ktb = al([2, 512], BF16); vtb = al([2, 512], BF16)
        s1 = al([528], F32); s2 = al([528], F32); s3 = s1; s4 = s2
        ktok = al([256], F32); vtok = al([256], F32); vtokb = al([256], BF16)
        sm = {k: al([8], F32) for k in ("ab", "beta", "g", "gc", "gtot", "egc", "ekd", "t4", "ssq", "rr")}
        XL = al([4, 128], F32); dg = XL; XU = al([4, 128], F32); EL = XL; EU = XU
        ER = XL
        Mb = [al([4, 128], F32) for _ in range(2)]; MT = [al([4, 128], F32) for _ in range(6)]
        attnT = al([4, 128], BF16); r32 = al([4, 128], F32); rbf = r32
        wtok = al([2, 128], BF16); wT = al([2, 128], BF16); qdT = al([2, 128], BF16); kupd = al([2, 128], BF16)
        vnew = al([256], BF16); Ghd = al([2, 128], F32); glast = al([2, 2], F32)
        kvt = al([512], BF16)
        otok = f1[0]; osq = f1[2]; sgate = f1[3]; ytok = al([256], BF16); tmpS = al([2, 128], F32)
        identF = C["ident"].ap; identB = CB["ident"].ap

        dma("sp", win.ap, wb_in[l], reads=[T_w], writes=[win], key="win")
        for j in range(31):
            for c in range(2):
                ts(cdg.ap[:, j * 2 + c, :], identF, pv.ap[:, l, 74 + j * 2 + c:75 + j * 2 + c], ALU.mult, reads=[C["ident"], pv], writes=[cdg])
        for j in range(4):
            for i in range(6):
                ts(ddg.ap[:, j * 6 + i, :], identF, pv.ap[:, l, 136 + j * 6 + i:137 + j * 6 + i], ALU.mult, reads=[C["ident"], pv], writes=[ddg])
        mset(wpf.ap, 0.0, writes=[wpf])
        for gq in range(4):
            h0 = (gq % 2) * 64
            dma("sp", wpf.ap[h0:h0 + 64, gq // 2, h0:h0 + 64], pool_w[l, gq], writes=[wpf], key="wp")
        cp(wpb.ap, wpf.ap, reads=[wpf], writes=[wpb])

        def proj(c0, ncol, N):
            b = bank()
            for k in range(8):
                mm(b.ap[0:ncol, 0:N], win.ap[:, k, c0:c0 + ncol], hT.ap[:, k, 0:N], start=(k == 0), stop=(k == 7), reads=[win, hT], writes=[b])
            return b

        def load_prefix(src, nrow, ncol, dst, c0, nchunk, isbf):
            dma("sp", stg.ap[0:nrow, 0:ncol], src, writes=[stg], key="pf")
            for i in range(nchunk):
                b = bank()
                tr(b.ap[:, 0:nrow], stg.ap[0:nrow, i * 128:(i + 1) * 128], identF[0:nrow, 0:nrow], reads=[stg, C["ident"]], writes=[b])
                cp(dst.ap[:, i, c0:c0 + nrow], b.ap[:, 0:nrow], reads=[b], writes=[dst])

        def store_tail(dst_dram, nrow, src, cs, nchunk, isbf, r0=0):
            for i in range(nchunk):
                b = PSB if isbf else bank()
                if isbf:
                    tr(bfv(b)[0:nrow, 0:128], src.ap[:, i, cs:cs + nrow], identB, reads=[src, CB["ident"]], writes=[b])
                    cp(stg.ap[0:nrow, i * 128:(i + 1) * 128], bfv(b)[0:nrow, 0:128], reads=[b], writes=[stg])
                else:
                    tr(b.ap[0:nrow, 0:128], src.ap[:, i, cs:cs + nrow], identF, reads=[src, C["ident"]], writes=[b])
                    cp(stg.ap[0:nrow, i * 128:(i + 1) * 128], b.ap[0:nrow, 0:128], reads=[b], writes=[stg])
            dma("pool", dst_dram, stg.ap[r0:nrow, 0:nchunk * 128], reads=[stg], writes=[T_out], key="tl")

        def headnorm(src, N, bias_col, scale, dst, gcol=None):
            sq = b1[0]
            tt(sq.ap[:, 0:N], src.ap[:, 0:N], src.ap[:, 0:N], ALU.mult, reads=[src], writes=[sq])
            b = bank()
            mm(b.ap[:, 0:N], CB["blk64"].ap, sq.ap[:, 0:N], reads=[sq, CB["blk64"]], writes=[b])
            rt = f1[5]
            act(rt.ap[:, 0:N], b.ap[:, 0:N], AF.Sqrt, bias=bias_col, scale=scale, reads=[b, epsc], writes=[rt])
            recip(rt.ap[:, 0:N], rt.ap[:, 0:N], reads=[rt], writes=[rt])
            if gcol is None:
                tt(dst, src.ap[:, 0:N], rt.ap[:, 0:N], ALU.mult, reads=[src, rt], writes=[])
            else:
                stt(dst, src.ap[:, 0:N], gcol, rt.ap[:, 0:N], ALU.mult, ALU.mult, reads=[src, rt, pv], writes=[])

        for (bq, col0, N, first, last, ui) in units:
            sidx = bq - 1
            if first:
                mset(aext.ap[:, :, 0:32], 0.0, writes=[aext]); mset(pext.ap[:, :, 0:16], 0.0, writes=[pext])
                mset(dnext.ap[:, :, 0:4], 0.0, writes=[dnext]); mset(S32.ap, 0.0, writes=[S32])
                if bq > 0:
                    load_prefix(cache_conv[l, sidx], 30, 256, aext, 2, 2, True)
                    load_prefix(cache_pool[l, sidx], 15, 256, pext, 1, 2, False)
                    load_prefix(cache_dnc[l, sidx], 3, 768, dnext, 1, 6, True)
                    for h in range(4):
                        h0 = (h % 2) * 64
                        dma("sp", S32.ap[h0:h0 + 64, h // 2, h0:h0 + 64], state_dn[l, sidx, h], writes=[S32], key="sd")
                cp(Sbf.ap, S32.ap, reads=[S32], writes=[Sbf])
            else:
                cp(aext.ap[:, :, 0:32], aext.ap[:, :, 512:544], reads=[aext], writes=[aext])
                cp(pext.ap[:, :, 0:16], pext.ap[:, :, 512:528], reads=[pext], writes=[pext])
                cp(dnext.ap[:, :, 0:4], dnext.ap[:, :, 512:516], reads=[dnext], writes=[dnext])
            dma("sp", x.ap[:, :, 0:N], xT_v[:, :, col0:col0 + N], reads=[T_xT[ui]], writes=[x], key="ax")
            rms_h(l, 0, bq, x, N, hT, x2, rs, tmps)
            parts = (dbg or {}).get('parts', 'conv pool sb dn')
            T = min(128, N)
            if 'conv' in parts:
                for c in range(2):
                    pvv = proj(c * 128, 128, N); pgg = proj(256 + c * 128, 128, N)
                    sg_ = f1[c]
                    act(sg_.ap[:, 0:N], pgg.ap[:, 0:N], AF.Sigmoid, reads=[pgg], writes=[sg_])
                    tt(aext.ap[:, c, 32:32 + N], pvv.ap[:, 0:N], sg_.ap[:, 0:N], ALU.mult, reads=[pvv, sg_], writes=[aext])
                for c in range(2):
                    b = bank()
                    for j in range(31):
                        mm(b.ap[:, 0:N], cdg.ap[:, j * 2 + c, :], aext.ap[:, c, 2 + j:2 + j + N], start=(j == 0), stop=(j == 30), reads=[cdg, aext], writes=[b])
                    act(ycv.ap[:, c, 0:N], b.ap[:, 0:N], AF.Identity, bias=pv.ap[:, l, 64 + c:65 + c], scale=1.0, reads=[b, pv], writes=[ycv])
                for c in range(2):
                    tt(f1[c].ap[:, 0:N], ycv.ap[:, c, 0:N], ycv.ap[:, c, 0:N], ALU.mult, reads=[ycv], writes=[f1[c]])
                bm = bank(); bs_ = bank()
                for c in range(2):
                    mm(bm.ap[:, 0:N], C["ones"].ap, ycv.ap[:, c, 0:N], start=(c == 0), stop=(c == 1), reads=[ycv, C["ones"]], writes=[bm])
                for c in range(2):
                    mm(bs_.ap[:, 0:N], C["ones"].ap, f1[c].ap[:, 0:N], start=(c == 0), stop=(c == 1), reads=[f1[c], C["ones"]], writes=[bs_])
                mean = f1[2]; var = f1[3]
                act(mean.ap[:, 0:N], bm.ap[:, 0:N], AF.Identity, scale=1.0 / 256, reads=[bm], writes=[mean])
                tt(var.ap[:, 0:N], mean.ap[:, 0:N], mean.ap[:, 0:N], ALU.mult, reads=[mean], writes=[var])
                stt(var.ap[:, 0:N], bs_.ap[:, 0:N], 1.0 / 256, var.ap[:, 0:N], ALU.mult, ALU.subtract, reads=[bs_, var], writes=[var])
                act(var.ap[:, 0:N], var.ap[:, 0:N], AF.Sqrt, bias=epsc.ap[:, 0:1], scale=1.0, reads=[var, epsc], writes=[var])
                recip(var.ap[:, 0:N], var.ap[:, 0:N], reads=[var], writes=[var])
                for c in range(2):
                    t1 = f1[4]
                    tt(t1.ap[:, 0:N], ycv.ap[:, c, 0:N], mean.ap[:, 0:N], ALU.subtract, reads=[ycv, mean], writes=[t1])
                    stt(t1.ap[:, 0:N], t1.ap[:, 0:N], pv.ap[:, l, 66 + c:67 + c], var.ap[:, 0:N], ALU.mult, ALU.mult, reads=[t1, pv, var], writes=[t1])
                    act(mixb.ap[:, c, 0:N], t1.ap[:, 0:N], AF.Silu, bias=pv.ap[:, l, 68 + c:69 + c], scale=1.0, reads=[t1, pv], writes=[mixb])
                if last:
                    dst = o_conv_p[l] if bq == 0 else o_conv_s[l, sidx]
                    store_tail(dst, 30, aext, 32 + N - 30, 2, True)
            if 'pool' in parts:
                for c in range(2):
                    pp = proj(1544 + c * 128, 128, N)
                    cp(pext.ap[:, c, 16:16 + N], pp.ap[:, 0:N], reads=[pp], writes=[pext])
                W = 16 + N
                for c in range(2):
                    xe = pext.ap[:, c, :]
                    tt(s1.ap[:, 1:W], xe[:, 1:W], xe[:, 0:W - 1], ALU.add, reads=[pext], writes=[s1])
                    tt(s2.ap[:, 3:W], s1.ap[:, 3:W], s1.ap[:, 1:W - 2], ALU.add, reads=[s1], writes=[s2])
                    if c == 1:
                        tt(s3.ap[:, 7:W], s2.ap[:, 7:W], s2.ap[:, 3:W - 4], ALU.add, reads=[s2], writes=[s3])
                        tt(s4.ap[:, 15:W], s3.ap[:, 15:W], s3.ap[:, 7:W - 8], ALU.add, reads=[s3], writes=[s4])
                    lo, hi = (s1, s2) if c == 0 else (s3, s4)
                    pl = b1[1]
                    for (win_, p0) in ((lo, 0), (hi, 64)):
                        stt(pl.ap[p0:p0 + 64, 0:N], win_.ap[p0:p0 + 64, 16:16 + N], C["invw"].ap[p0:p0 + 64, c:c + 1], xe[p0:p0 + 64, 16:16 + N],
                            ALU.mult, ALU.subtract, reads=[win_, C["invw"], pext], writes=[pl])
                        if first and bq == 0:
                            t16 = f1[4]
                            tt(t16.ap[p0:p0 + 64, 0:16], win_.ap[p0:p0 + 64, 16:32], C["invcnt0"].ap[p0:p0 + 64, c, :], ALU.mult, reads=[win_, C["invcnt0"]], writes=[t16])
                            tt(pl.ap[p0:p0 + 64, 0:16], t16.ap[p0:p0 + 64, 0:16], xe[p0:p0 + 64, 16:32], ALU.subtract, reads=[t16, pext], writes=[pl])
                    b = bank()
                    mm(b.ap[:, 0:N], wpb.ap[:, c, :], pl.ap[:, 0:N], reads=[wpb, pl], writes=[b])
                    act(mixb.ap[:, 4 + c, 0:N], b.ap[:, 0:N], AF.Identity, scale=pv.ap[:, l, 70 + c:71 + c], reads=[b, pv], writes=[mixb])
                if last:
                    dst = o_pool_p[l] if bq == 0 else o_pool_s[l, sidx]
                    store_tail(dst, 15, pext, 16 + N - 15, 2, False)
            if 'sb' in parts:
                for i in range(6):
                    pq = proj(1800 + i * 128, 128, N)
                    if i < 4:
                        xs_ = f1[0]
                        act(xs_.ap[:, 0:N], pq.ap[:, 0:N], AF.Copy, reads=[pq], writes=[xs_])
                        if i < 2:
                            headnorm(xs_, N, epsc.ap[:, 1:2], 1.0, mixb.ap[:, 6 + i, 0:N], gcol=pv.ap[:, l, 72:73])
                            mixb.w = ("dve", S.cnt["dve"]); mixb.r = {}
                        else:
                            headnorm(xs_, N, epsc.ap[:, 0:1], 1.0 / 64, ktb.ap[:, i - 2, 0:N], gcol=pv.ap[:, l, 73:74])
                            ktb.w = ("dve", S.cnt["dve"]); ktb.r = {}
                    else:
                        act(vtb.ap[:, i - 4, 0:N], pq.ap[:, 0:N], AF.Copy, reads=[pq], writes=[vtb])
                sbl = (dbg or {}).get('sbl', 9)
                if sbl >= 2:
                    dma((dbg or {}).get('ksq', 'pool'), kT_scr[:, :, col0:col0 + N], ktb.ap[:, :, 0:N], reads=[ktb], writes=[T_kv], key="ks")
                T = min(128, N)
                for t0 in range(0, N if sbl >= 3 else 0, T):
                    b = PSB; bb = bfv(b)
                    for p in range(2):
                        tr(bb[0:T, p * 128:(p + 1) * 128], ktb.ap[:, p, t0:t0 + T], identB, reads=[ktb, CB["ident"]], writes=[b])
                        tr(bb[0:T, 256 + p * 128:256 + (p + 1) * 128], vtb.ap[:, p, t0:t0 + T], identB, reads=[vtb, CB["ident"]], writes=[b])
                    cp(ktok.ap[0:T, :], bb[0:T, 0:256], reads=[b], writes=[ktok])
                    act(vtok.ap[0:T, :], bb[0:T, 256:512], AF.Copy, reads=[b], writes=[vtok])
                    cp(vtokb.ap[0:T, :], bb[0:T, 256:512], reads=[b], writes=[vtokb])
                    if sbl < 4:
                        continue
                    if bq == 0:
                        dk = o_k_p[l, :, col0 + t0:col0 + t0 + T, :]; dv = o_v_p[l, :, col0 + t0:col0 + t0 + T, :]
                    else:
                        dk = o_k_s[l, sidx]; dv = o_v_s[l, sidx]
                    dma("pool", dk.rearrange("h t d -> t h d"), ktok.ap[0:T, :].rearrange("t (h d) -> t h d", h=4), reads=[ktok], writes=[T_out], key="ok")
                    dma("pool", dv.rearrange("h t d -> t h d"), vtok.ap[0:T, :].rearrange("t (h d) -> t h d", h=4), reads=[vtok], writes=[T_out], key="ov")
                    dma("pool", v_scr[col0 + t0:col0 + t0 + T, :], vtokb.ap[0:T, :], reads=[vtokb], writes=[T_kv], key="vs")
            if 'dn' in parts:
                for i in range(6):
                    pq = proj(512 + i * 128, 128, N)
                    cp(dnext.ap[:, i, 4:4 + N], pq.ap[:, 0:N], reads=[pq], writes=[dnext])
                if last:
                    dst = o_dnc_p[l] if bq == 0 else o_dnc_s[l, sidx]
                    store_tail(dst, 4, dnext, 4 + N - 4, 6, True, r0=1)
                for i in range(6):
                    b = bank()
                    for j in range(4):
                        mm(b.ap[:, 0:N], ddg.ap[:, j * 6 + i, :], dnext.ap[:, i, 1 + j:1 + j + N], start=(j == 0), stop=(j == 3), reads=[ddg, dnext], writes=[b])
                    if i < 4:
                        sl_ = f1[1]
                        act(sl_.ap[:, 0:N], b.ap[:, 0:N], AF.Silu, reads=[b], writes=[sl_])
                        if i < 2:
                            headnorm(sl_, N, epsc.ap[:, 1:2], 64.0, dqk.ap[:, i, 0:N])
                        else:
                            headnorm(sl_, N, epsc.ap[:, 0:1], 1.0, dqk.ap[:, i, 0:N])
                        dqk.w = ("dve", S.cnt["dve"]); dqk.r = {}
                    else:
                        act(dqk.ap[:, i, 0:N], b.ap[:, 0:N], AF.Silu, reads=[b], writes=[dqk])
                for t0 in range(0, N, T):
                    dn_tile(l, bq, col0, N, t0, T, locals())
                if last:
                    dst = o_dn_p[l] if bq == 0 else o_dn_s[l, sidx]
                    for h in range(4):
                        h0 = (h % 2) * 64
                        dma("pool", dst[h], S32.ap[h0:h0 + 64, h // 2, h0:h0 + 64], reads=[S32], writes=[T_out], key="so")
            dma("pool", mixq[:, :, col0:col0 + N], mixb.ap[:, :, 0:N], reads=[mixb], writes=[T_mixq[ui]], key="mq")

    def dn_tile(l, bq, col0, N, t0, T, L):
        (hT, win, sm, dqk, dg, XL, XU, EL, EU, ER, Mb, MT, attnT, r32, rbf, wtok, wT, qdT, kupd, vnew, Ghd, glast, otok, osq, sgate,
         ytok, tmpS, S32, Sbf, mixb, identF, identB, kvt) = [L[k] for k in (
            "hT", "win", "sm", "dqk", "dg", "XL", "XU", "EL", "EU", "ER", "Mb", "MT", "attnT", "r32", "rbf", "wtok", "wT", "qdT", "kupd",
            "vnew", "Ghd", "glast", "otok", "osq", "sgate", "ytok", "tmpS", "S32", "Sbf", "mixb", "identF", "identB", "kvt")]
        tc = slice(t0, t0 + T)
        nch = T // 64
        pab = bank()
        for k in range(8):
            mm(pab.ap[0:T, 0:8], hT.ap[:, k, tc], win.ap[:, k, 1536:1544], start=(k == 0), stop=(k == 7), reads=[hT, win], writes=[pab])
        pgt = bank()
        for k in range(8):
            mm(pgt.ap[0:T, 0:256], hT.ap[:, k, tc], win.ap[:, k, 1280:1536], start=(k == 0), stop=(k == 7), reads=[hT, win], writes=[pgt])
        act(sgate.ap[0:T, 0:256], pgt.ap[0:T, 0:256], AF.Silu, reads=[pgt], writes=[sgate])
        beta, g_, gc, gtot, egc, ekd, t4, ssq, rr = [sm[k] for k in ("beta", "g", "gc", "gtot", "egc", "ekd", "t4", "ssq", "rr")]
        act(beta.ap[0:T, 0:4], pab.ap[0:T, 4:8], AF.Sigmoid, reads=[pab], writes=[beta])
        tt(t4.ap[0:T, 0:4], pab.ap[0:T, 0:4], rowb.ap[0:T, l, 256:260], ALU.add, reads=[pab, rowb], writes=[t4])
        act(t4.ap[0:T, 0:4], t4.ap[0:T, 0:4], AF.Exp, reads=[t4], writes=[t4])
        act(t4.ap[0:T, 0:4], t4.ap[0:T, 0:4], AF.Ln, bias=epsc.ap[0:T, 2:3], scale=1.0, reads=[t4, epsc], writes=[t4])
        tt(g_.ap[0:T, 0:4], t4.ap[0:T, 0:4], negA.ap[0:T, l, :], ALU.mult, reads=[t4, negA], writes=[g_])
        b = bank()
        mm(b.ap[0:T, 0:4], C["ublk"].ap[0:T, 0:T], g_.ap[0:T, 0:4], reads=[C["ublk"], g_], writes=[b])
        mm(b.ap[0:T, 8:12], C["blk64"].ap[0:T, 0:T], g_.ap[0:T, 0:4], reads=[C["blk64"], g_], writes=[b])
        cp(gc.ap[0:T, 0:4], b.ap[0:T, 0:4], reads=[b], writes=[gc])
        act(egc.ap[0:T, 0:4], b.ap[0:T, 0:4], AF.Exp, reads=[b], writes=[egc])
        tt(ekd.ap[0:T, 0:4], b.ap[0:T, 8:12], gc.ap[0:T, 0:4], ALU.subtract, reads=[b, gc], writes=[ekd])
        act(ekd.ap[0:T, 0:4], ekd.ap[0:T, 0:4], AF.Exp, reads=[ekd], writes=[ekd])
        bt = PSB; btb = bfv(bt)
        for p in range(2):
            tr(btb[0:T, p * 128:(p + 1) * 128], dqk.ap[:, 2 + p, tc], identB, reads=[dqk, CB["ident"]], writes=[bt])
            tr(btb[0:T, 256 + p * 128:256 + (p + 1) * 128], dqk.ap[:, 4 + p, tc], identB, reads=[dqk, CB["ident"]], writes=[bt])
        cp(kvt.ap[0:T, :], btb[0:T, 0:512], reads=[bt], writes=[kvt])
        btb = kvt.ap; bt = kvt
        bKK = bank(); bQK = bank(); bR = bank()
        for h in range(4):
            p, hs = h // 2, slice((h % 2) * 64, (h % 2) * 64 + 64)
            mm(bKK.ap[0:T, h * 128:h * 128 + T], dqk.ap[hs, 2 + p, tc], dqk.ap[hs, 2 + p, tc], reads=[dqk], writes=[bKK])
            mm(bQK.ap[0:T, h * 128:h * 128 + T], dqk.ap[hs, 2 + p, tc], dqk.ap[hs, p, tc], reads=[dqk], writes=[bQK])
            ts(dg.ap[0:T, h, 0:T], identF[0:T, 0:T], gc.ap[0:T, h:h + 1], ALU.mult, reads=[C["ident"], gc], writes=[dg])
            mm(bR.ap[:, h * 128:h * 128 + T], C["ones"].ap[0:T, :], dg.ap[0:T, h, 0:T], reads=[C["ones"], dg], writes=[bR])
        for h in range(4):
            stt(XL.ap[0:T, h, 0:T], bR.ap[0:T, h * 128:h * 128 + T], gc.ap[0:T, h:h + 1], C["pmL"].ap[0:T, 0:T], ALU.subtract, ALU.add,
                reads=[bR, gc, C["pmL"]], writes=[XL])
            stt(XU.ap[0:T, h, 0:T], bR.ap[0:T, h * 128:h * 128 + T], gc.ap[0:T, h:h + 1], C["nmU"].ap[0:T, 0:T], ALU.subtract, ALU.add,
                reads=[bR, gc, C["nmU"]], writes=[XU])
        act(EL.ap[0:T, :, 0:T], XL.ap[0:T, :, 0:T], AF.Exp, scale=-1.0, reads=[XL], writes=[EL])
        act(EU.ap[0:T, :, 0:T], XU.ap[0:T, :, 0:T], AF.Exp, reads=[XU], writes=[EU])
        M0 = Mb[0]
        for h in range(4):
            stt(M0.ap[0:T, h, 0:T], bKK.ap[0:T, h * 128:h * 128 + T], beta.ap[0:T, h:h + 1], EL.ap[0:T, h, 0:T], ALU.mult, ALU.mult,
                reads=[bKK, beta, EL], writes=[M0])
        tt(attnT.ap[0:T, :, 0:T], bQK.ap[:, :].rearrange("p (h t) -> p h t", h=4)[0:T, :, 0:T], EU.ap[0:T, :, 0:T], ALU.mult,
           reads=[bQK, EU], writes=[attnT])
        act(ER.ap[:, :, 0:T], bR.ap[:, :].rearrange("p (h t) -> p h t", h=4)[:, :, 0:T], AF.Exp, reads=[bR], writes=[ER])
        bL = bank()
        for h in range(4):
            tr(bL.ap[0:T, h * 128:h * 128 + T], M0.ap[0:T, h, 0:T], identF[0:T, 0:T], reads=[M0, C["ident"]], writes=[bL])
        cp(MT[0].ap[0:T, :, 0:T], bL.ap[:, :].rearrange("p (h t) -> p h t", h=4)[0:T, :, 0:T], reads=[bL], writes=[MT[0]])
        Mc = M0
        for lev in range(1, 6):
            bA = bank(); bB = bank()
            Mn = Mb[lev % 2]
            for h in range(4):
                mm(bA.ap[0:T, h * 128:h * 128 + T], MT[lev - 1].ap[0:T, h, 0:T], Mc.ap[0:T, h, 0:T], reads=[MT[lev - 1], Mc], writes=[bA])
                mm(bB.ap[0:T, h * 128:h * 128 + T], Mc.ap[0:T, h, 0:T], MT[lev - 1].ap[0:T, h, 0:T], reads=[MT[lev - 1], Mc], writes=[bB])
            if lev < 5:
                cp(Mn.ap[0:T, :, 0:T], bA.ap[:, :].rearrange("p (h t) -> p h t", h=4)[0:T, :, 0:T], reads=[bA], writes=[Mn])
            act(MT[lev].ap[0:T, :, 0:T], bB.ap[:, :].rearrange("p (h t) -> p h t", h=4)[0:T, :, 0:T], AF.Copy, reads=[bB], writes=[MT[lev]])
            Mc = Mn
        for h in range(4):
            ts(r32.ap[0:T, h, 0:64], btb[0:T, 256 + h * 64:256 + (h + 1) * 64], beta.ap[0:T, h:h + 1], ALU.mult, reads=[bt, beta], writes=[r32])
            ts(r32.ap[0:T, h, 64:128], btb[0:T, h * 64:(h + 1) * 64], beta.ap[0:T, h:h + 1], ALU.mult, egc.ap[0:T, h:h + 1], ALU.mult,
               reads=[bt, beta, egc], writes=[r32])
            ts(kupd.ap[0:T, h // 2, (h % 2) * 64:(h % 2) * 64 + 64], btb[0:T, h * 64:(h + 1) * 64], ekd.ap[0:T, h:h + 1], ALU.mult,
               reads=[bt, ekd], writes=[kupd])
        for lev in range(6):
            bx = bank()
            for h in range(4):
                mm(bx.ap[0:T, h * 128:(h + 1) * 128], MT[lev].ap[0:T, h, 0:T], rbf.ap[0:T, h, :], reads=[MT[lev], rbf], writes=[bx])
            tt(r32.ap[0:T], r32.ap[0:T], bx.ap[:, :].rearrange("p (h t) -> p h t", h=4)[0:T], ALU.subtract if lev == 0 else ALU.add,
               reads=[r32, bx], writes=[r32])
        cp(wtok.ap[0:T].rearrange("t p (h d) -> t (p h) d", h=2), rbf.ap[0:T, :, 64:128], reads=[rbf], writes=[wtok])
        bw = PSB; bwb = bfv(bw)
        for p in range(2):
            tr(bwb[:, p * 128:p * 128 + T], wtok.ap[0:T, p, :], identB[0:T, 0:T], reads=[wtok, CB["ident"]], writes=[bw])
        cp(wT.ap[:, :, 0:T], bwb[:, 0:256].rearrange("p (a t) -> p a t", a=2)[:, :, 0:T], reads=[bw], writes=[wT])
        for h in range(4):
            p, hs = h // 2, slice((h % 2) * 64, (h % 2) * 64 + 64)
            tt(qdT.ap[hs, p, 0:T], dqk.ap[hs, p, tc], ER.ap[hs, h, 0:T], ALU.mult, reads=[dqk, ER], writes=[qdT])
        for h in range(4):
            ts(Ghd.ap[0:T, h // 2, (h % 2) * 64:(h % 2) * 64 + 64], C["ones"].ap[0:T, 0:64], g_.ap[0:T, h:h + 1], ALU.mult,
               reads=[C["ones"], g_], writes=[Ghd])
        bg = bank()
        for p in range(2):
            mm(bg.ap[:, p * 2:p * 2 + 2], Ghd.ap[0:T, p, :], C["chunkind"].ap[0:T, :], reads=[Ghd, C["chunkind"]], writes=[bg])
        act(glast.ap, bg.ap[:, 0:4].rearrange("p (a c) -> p a c", a=2), AF.Exp, reads=[bg], writes=[glast])
        bO = PS_DED
        for ci in range(nch):
            cs = slice(ci * 64, ci * 64 + 64)
            bV = bank()
            for p in range(2):
                mm(bV.ap[cs, p * 128:(p + 1) * 128], wT.ap[:, p, cs], Sbf.ap[:, p, :], reads=[wT, Sbf], writes=[bV])
            tt(vnew.ap[cs, :].rearrange("c (h e) -> c h e", h=4), r32.ap[cs, :, 0:64], bV.ap[cs, 0:256].rearrange("c (h e) -> c h e", h=4),
               ALU.subtract, reads=[r32, bV], writes=[vnew])
            for p in range(2):
                mm(bO.ap[cs, p * 128:(p + 1) * 128], qdT.ap[:, p, cs], Sbf.ap[:, p, :], start=True, stop=False, reads=[qdT, Sbf], writes=[bO])
                for hh in range(2):
                    h = 2 * p + hh
                    mm(bO.ap[cs, h * 64:(h + 1) * 64], attnT.ap[cs, h, cs], vnew.ap[cs, h * 64:(h + 1) * 64], start=False, stop=(hh == 1),
                       reads=[attnT, vnew], writes=[bO])
            for p in range(2):
                bS = bank()
                mm(bS.ap[:, 0:128], kupd.ap[cs, p, :], vnew.ap[cs, p * 128:(p + 1) * 128], reads=[kupd, vnew], writes=[bS])
                tt(tmpS.ap[:, p, :], bS.ap[:, 0:128], C["blk64"].ap, ALU.mult, reads=[bS, C["blk64"]], writes=[tmpS])
                stt(S32.ap[:, p, :], S32.ap[:, p, :], glast.ap[:, p, ci:ci + 1], tmpS.ap[:, p, :], ALU.mult, ALU.add,
                    reads=[S32, glast, tmpS], writes=[S32])
            act(Sbf.ap, S32.ap, AF.Copy, reads=[S32], writes=[Sbf])
        cp(otok.ap[0:T, 0:256], bO.ap[0:T, 0:256], reads=[bO], writes=[otok])
        tt(osq.ap[0:T, 0:256], otok.ap[0:T, 0:256], otok.ap[0:T, 0:256], ALU.mult, reads=[otok], writes=[osq])
        red(ssq.ap[0:T, 0:4], osq.ap[0:T, 0:256].rearrange("t (h e) -> t h e", h=4), reads=[osq], writes=[ssq])
        act(rr.ap[0:T, 0:4], ssq.ap[0:T, 0:4], AF.Sqrt, bias=epsc.ap[0:T, 0:1], scale=1.0 / 64, reads=[ssq, epsc], writes=[rr])
        recip(rr.ap[0:T, 0:4], rr.ap[0:T, 0:4], reads=[rr], writes=[rr])
        for h in range(4):
            ts(osq.ap[0:T, h * 64:(h + 1) * 64], otok.ap[0:T, h * 64:(h + 1) * 64], rr.ap[0:T, h:h + 1], ALU.mult, reads=[otok, rr], writes=[osq])
        tt(osq.ap[0:T, 0:256], osq.ap[0:T, 0:256], rowb.ap[0:T, l, 0:256], ALU.mult, reads=[osq, rowb], writes=[osq])
        tt(ytok.ap[0:T, :], osq.ap[0:T, 0:256], sgate.ap[0:T, 0:256], ALU.mult, reads=[osq, sgate], writes=[ytok])
        by = PSB; byb = bfv(by)
        for p in range(2):
            tr(byb[:, p * 128:p * 128 + T], ytok.ap[0:T, p * 128:(p + 1) * 128], identB[0:T, 0:T], reads=[ytok, CB["ident"]], writes=[by])
        cp(mixb.ap[:, 2:4, tc], byb[:, 0:256].rearrange("p (a t) -> p a t", a=2)[:, :, 0:T], reads=[by], writes=[mixb])

    def layer_A2(l):
        S.barrier()
        A.off = A_persist
        al = A.alloc
        KMAX = max(SEQ, 1152); NB = KMAX // 128
        wout = al([8, D], BF16)
        kT = al([2, KMAX], BF16); vS = al([NB, 256], BF16)
        kst = al([8, 256], F32)
        mixb = al([8, 512], BF16); x = al([8, 512], F32)
        NR = 3
        eb = [al([2, 512], F32) for _ in range(NR)]; spb = [al([2, 512], BF16) for _ in range(NR)]
        wb_ = [al([2, 512], F32) for _ in range(NR)]; ab_ = [al([2, 512], BF16) for _ in range(NR)]
        acc = al([2, 512], BF16)
        ROT[:] = [PSs[0], PSs[1]]
        identF = C["ident"].ap
        dma("sp", wout.ap, wb_out[l], reads=[T_w], writes=[wout], key="wo")
        it = 0
        prompt_loaded = False
        for (bq, col0, N, first, last, ui) in units:
            sidx = bq - 1
            if bq > 0:
                for h in range(4):
                    dma("sp", kst.ap[:, :, h * 64:(h + 1) * 64], cache_v[l, sidx, h].rearrange("(n p) d -> p n d", p=128), writes=[kst], key="cv")
                cp(vS.ap[:, 0:8, :], kst.ap, reads=[kst], writes=[vS])
                dma("sp", vS.ap[0:64, 8, :], v_scr[col0:col0 + 64, :], reads=[T_kv], writes=[vS], key="cv2")
                for h in range(4):
                    dma("sp", kst.ap[:, :, h * 64:(h + 1) * 64], cache_k[l, sidx, h].rearrange("(n p) d -> p n d", p=128), writes=[kst], key="ck")
                for p in range(2):
                    for n0 in range(0, 8, 4):
                        b = bank()
                        for n in range(4):
                            tr(b.ap[:, n * 128:(n + 1) * 128], kst.ap[:, n0 + n, p * 128:(p + 1) * 128], identF, reads=[kst, C["ident"]], writes=[b])
                        cp(kT.ap[:, p, n0 * 128:(n0 + 4) * 128], b.ap[:, :], reads=[b], writes=[kT])
                dma("sp", kT.ap[:, :, 1024:1088], kT_scr[:, :, col0:col0 + 64], reads=[T_kv], writes=[kT], key="ck2")
                blocks = [(kb, 128, None) for kb in range(8)] + [(8, 64, 0)]
            else:
                if not prompt_loaded:
                    dma("sp", kT.ap[:, :, 0:SEQ], kT_scr[:, :, 0:SEQ], reads=[T_kv], writes=[kT], key="pk")
                    dma("sp", vS.ap[:, 0:SEQ // 128, :], v_scr[0:SEQ, :].rearrange("(n p) c -> p n c", p=128), reads=[T_kv], writes=[vS], key="pv")
                    prompt_loaded = True
                u = col0 // 512
                blocks = [(kb, 128, (kb - 4 * u) if kb >= 4 * u else None) for kb in range(4 * u + 4)]
            dma("sp", mixb.ap[:, :, 0:N], mixq[:, :, col0:col0 + N], reads=[T_mixq[ui]], writes=[mixb], key="mx")
            dma("sp", x.ap[:, :, 0:N], xT_v[:, :, col0:col0 + N], reads=[T_xT[ui]], writes=[x], key="a2x")
            for p in range(2):
                bo = PS_DED
                if bq > 0:
                    mset(acc.ap, 0.0, writes=[acc])
                nblk = len(blocks)
                for bi, (kb, KR, msk) in enumerate(reversed(blocks)):
                    isf = (bi == 0)
                    e_ = eb[it % NR]; sp_ = spb[it % NR]; w_ = wb_[it % NR]; a_ = ab_[it % NR]
                    bz = PD[it % 2]; bc = PD[(it + 1) % 2]
                    it += 1
                    bzv = bz.ap.rearrange("p (a n) -> p a n", a=2); bcv = bc.ap.rearrange("p (a n) -> p a n", a=2)
                    for hh in range(2):
                        hs = slice(hh * 64, hh * 64 + 64)
                        mm(bzv[0:KR, hh, 0:N], kT.ap[hs, p, kb * 128:kb * 128 + KR], mixb.ap[hs, 6 + p, 0:N], reads=[kT, mixb], writes=[bz])
                    act(e_.ap[0:KR, :, 0:N], bzv[0:KR, :, 0:N], AF.Exp, reads=[bz], writes=[e_])
                    if msk is not None:
                        for hh in range(2):
                            tt(e_.ap[0:KR, hh, 0:N], e_.ap[0:KR, hh, 0:N], CB["sbmask"].ap[0:KR, msk, 0:N], ALU.mult, reads=[e_, CB["sbmask"]], writes=[e_])
                    act(sp_.ap[0:KR, :, 0:N], e_.ap[0:KR, :, 0:N], AF.Ln, bias=epsc.ap[0:KR, 2:3], scale=1.0, reads=[e_, epsc], writes=[sp_])
                    for hh in range(2):
                        mm(bcv[0:KR, hh, 0:N], CB["upper"].ap[0:KR, 0:KR], sp_.ap[0:KR, hh, 0:N], start=True, stop=isf, reads=[CB["upper"], sp_], writes=[bc])
                        if not isf:
                            mm(bcv[0:KR, hh, 0:N], CB["ones"].ap[:, 0:KR], acc.ap[:, hh, 0:N], start=False, stop=True, reads=[CB["ones"], acc], writes=[bc])
                    act(w_.ap[0:KR, :, 0:N], bcv[0:KR, :, 0:N], AF.Exp, scale=-1.0, reads=[bc], writes=[w_])
                    tt(a_.ap[0:KR, :, 0:N], e_.ap[0:KR, :, 0:N], w_.ap[0:KR, :, 0:N], ALU.mult, reads=[e_, w_], writes=[a_])
                    if isf and bq == 0:
                        cp(acc.ap[0:KR, :, 0:N], sp_.ap[0:KR, :, 0:N], reads=[sp_], writes=[acc], eng="pool")
                    else:
                        tt(acc.ap[0:KR, :, 0:N], acc.ap[0:KR, :, 0:N], sp_.ap[0:KR, :, 0:N], ALU.add, reads=[acc, sp_], writes=[acc], eng="pool")
                    for hh in range(2):
                        h = 2 * p + hh
                        hs = slice(hh * 64, hh * 64 + 64)
                        mm(bo.ap[hs, 0:N], vS.ap[0:KR, kb, h * 64:(h + 1) * 64], a_.ap[0:KR, hh, 0:N], start=isf, stop=(bi == nblk - 1),
                           reads=[vS, a_], writes=[bo])
                cp(mixb.ap[:, 6 + p, 0:N], bo.ap[:, 0:N], reads=[bo], writes=[mixb])
            for m in range(8):
                po = bank()
                for k in range(8):
                    mm(po.ap[:, 0:N], wout.ap[:, k, m * 128:(m + 1) * 128], mixb.ap[:, k, 0:N], start=(k == 0), stop=(k == 7), reads=[wout, mixb], writes=[po])
                stt(x.ap[:, m, 0:N], po.ap[:, 0:N], modc(l, 2, m, bq), x.ap[:, m, 0:N], ALU.mult, ALU.add, reads=[po, mod, x], writes=[x])
            dma("pool", xT_v[:, :, col0:col0 + N], x.ap[:, :, 0:N], reads=[x], writes=[T_xT[ui]], key="a2o")

    for l in range(DEPTH):
        if dbg is not None and dbg.get("skip_layers"):
            break
        only = (dbg or {}).get("only", "A1A2B")
        if "A1" in only:
            layer_A1(l)
        if "A2" in only:
            layer_A2(l)
        if "B" in only:
            layer_B(l)

    S.barrier()
    A.off = A_persist
    ROT[:] = [PH[0], PH[1], PH[2], PH[3], PSs[0], PSs[1]]
    xin = [A.alloc([8, 128], F32) for _ in range(2)]
    xo = [A.alloc([1024], F32) for _ in range(2)]
    for t in range(ntile):
        dst = yp[t * 128:(t + 1) * 128, :] if t < SEQ // 128 else ys[(t - SEQ // 128) * 128:(t - SEQ // 128 + 1) * 128, :]
        xi = xin[t % 2]; xq = xo[t % 2]
        dma("sp", xi.ap, xT_v[:, :, t * 128:(t + 1) * 128], reads=T_xT, writes=[xi], key="yi%d" % (t % 2))
        for hlf in range(2):
            b = bank()
            for k in range(4):
                tr(b.ap[:, k * 128:(k + 1) * 128], xi.ap[:, hlf * 4 + k, :], C["ident"].ap, reads=[xi, C["ident"]], writes=[b])
            if hlf == 0:
                cp(xq.ap[:, 0:512], b.ap[:, :], reads=[b], writes=[xq])
            else:
                act(xq.ap[:, 512:1024], b.ap[:, :], AF.Copy, reads=[b], writes=[xq])
        dma("pool", dst, xq.ap, reads=[xq], writes=[T_out], key="yo%d" % (t % 2))
    S.barrier()
    S.emit()
    es.close()
    return nc


def host_layout(inp, DEPTH):
    pvrows = np.zeros((DEPTH, NPV, 128), np.float32)
    rowb = np.zeros((DEPTH, 128, 264), np.float32)
    for l in range(DEPTH):
        r = pvrows[l]
        r[0:48] = inp["b_ada"][l].reshape(48, 128)
        r[48:56] = inp["norm_mix"][l].reshape(8, 128)
        r[56:64] = inp["norm_ffn"][l].reshape(8, 128)
        r[64:66] = inp["conv_dw_b"][l].reshape(2, 128)
        r[66:68] = inp["conv_ln_g"][l].reshape(2, 128)
        r[68:70] = inp["conv_ln_b"][l].reshape(2, 128)
        r[70:72] = inp["pool_scale"][l].reshape(2, 128)
        r[72] = np.tile(inp["sb_q_norm"][l], 2)
        r[73] = np.tile(inp["sb_k_norm"][l], 2)
        r[74:136] = inp["conv_dw_w"][l].reshape(62, 128)
        r[136:160] = inp["dn_conv_w"][l].reshape(24, 128)
        rowb[l, :, 0:256] = np.tile(inp["dn_norm_g"][l], 4)[None, :]
        rowb[l, :, 256:260] = inp["dn_dt_bias"][l][None, :]
        rowb[l, :, 260:264] = inp["dn_a_log"][l][None, :]
    return pvrows, rowb


_NC_CACHE = {}


def run(inp, SEQ, DEPTH, ncores, dbg=None):
    inp = {k: np.ascontiguousarray(np.asarray(v, dtype=np.float32)) for k, v in inp.items()}
    key = (SEQ, DEPTH, str(dbg))
    if key not in _NC_CACHE:
        _NC_CACHE[key] = build(SEQ, DEPTH, dbg)
    nc = _NC_CACHE[key]
    pvrows, rowb = host_layout(inp, DEPTH)
    consts = host_consts()
    maps = []
    for i in range(ncores):
        sl = slice(4 * i, 4 * i + 4)
        m = {
            "xp": inp["x_prompt"][i], "xs": inp["x_sample"][sl].reshape(256, D),
            "c5": np.concatenate([inp["c_prompt"][i:i + 1], inp["c_sample"][sl]], 0),
            "cache_conv": inp["cache_conv"][:, sl], "state_dn": inp["state_dn"][:, sl],
            "cache_dn_conv": inp["cache_dn_conv"][:, sl], "cache_pool": inp["cache_pool"][:, sl],
            "cache_sb_k": inp["cache_sb_k"][:, sl], "cache_sb_v": inp["cache_sb_v"][:, sl],
            "w_ada": inp["w_ada"], "w_in": inp["w_in"], "w_out": inp["w_out"],
            "ffn_w_gate": inp["ffn_w_gate"], "ffn_w_up": inp["ffn_w_up"], "ffn_w_down": inp["ffn_w_down"],
            "pool_w": inp["pool_w"], "pvrows": pvrows, "rowb": rowb,
        }
        for k, v in consts.items():
            m["c_" + k] = v
        maps.append({k: np.ascontiguousarray(v) for k, v in m.items()})
    res = run_bass_kernel_spmd(nc, maps, core_ids=list(range(ncores)))
    R = res.results
    cat = lambda name, ax: np.concatenate([np.expand_dims(r[name], ax) if name.endswith("_p") or name == "yp" else r[name] for r in R], ax)
    y_p = np.stack([r["yp"] for r in R], 0)
    y_s = np.concatenate([r["ys"].reshape(4, 64, D) for r in R], 0)
    outs = [y_p, y_s]
    for nm in ("conv", "dn", "dnconv", "pool", "sbk", "sbv"):
        outs.append(np.stack([r[nm + "_p"] for r in R], 1))
        outs.append(np.concatenate([r[nm + "_s"] for r in R], 1))
    return tuple(np.ascontiguousarray(o.astype(np.float32)) for o in outs)


def kernel(**inputs):
    return run(inputs, 8192, 4, 8)
```
